# Optimizing a Trainium2 kernel written in Bass

```python
import math
import jax, jax.numpy as jnp
from jax import lax
import numpy as np

D_MODEL = 1024
BATCH = 8
SEQ = 2048
DEPTH = 2

MEM_LEN = 256
MLA_HEADS = 8
MLA_NOPE_DIM = 64
MLA_ROPE_DIM = 32
MLA_V_DIM = 64
Q_LORA_RANK = 384
KV_LORA_RANK = 256
ROPE_THETA = 10000.0
DIL_HEADS = 8
DIL_HEAD_DIM = 64
DIL_PATTERNS = ((128, 1), (512, 4), (2048, 16))
N_BUCKETS = 32
MAX_DISTANCE = 2048
X_HEADS = 4
X_HEAD_DIM = D_MODEL // X_HEADS
D_FF = 4 * D_MODEL

Q_BLOCK = 128
EPS = 1e-6
MLA_WIDTH = MLA_HEADS * MLA_V_DIM
DIL_WIDTH = DIL_HEADS * DIL_HEAD_DIM
MIX_WIDTH = MLA_WIDTH + DIL_WIDTH
MLA_QK_DIM = MLA_NOPE_DIM + MLA_ROPE_DIM
IN_WIDTH = Q_LORA_RANK + KV_LORA_RANK + MLA_ROPE_DIM + 3 * DIL_WIDTH

kernel_name = "hymba_mla_dilated_hybrid_block"


def rms_norm(x, g):
    xf = x.astype(jnp.float32)
    y = xf * lax.rsqrt(jnp.mean(xf * xf, axis=-1, keepdims=True) + EPS)
    return (y * g.astype(jnp.float32)).astype(x.dtype)


def rope(x, positions):
    half = x.shape[-1] // 2
    inv = ROPE_THETA ** (-jnp.arange(half, dtype=jnp.float32) / half)
    ang = positions.astype(jnp.float32)[..., None] * inv
    cos = jnp.cos(ang)[:, :, None, :]
    sin = jnp.sin(ang)[:, :, None, :]
    x1 = x[..., :half].astype(jnp.float32)
    x2 = x[..., half:].astype(jnp.float32)
    out = jnp.concatenate([x1 * cos - x2 * sin, x2 * cos + x1 * sin], axis=-1)
    return out.astype(x.dtype)


def t5_bucket(dist):
    max_exact = N_BUCKETS // 2
    d = jnp.maximum(dist, 0)
    df = jnp.maximum(d.astype(jnp.float32), 1.0)
    large = max_exact + (jnp.log(df / max_exact) / math.log(MAX_DISTANCE / max_exact)
                         * (N_BUCKETS - max_exact)).astype(jnp.int32)
    large = jnp.minimum(large, N_BUCKETS - 1)
    return jnp.where(d < max_exact, d, large)


def causal_block_attention(q, k, v):
    B, S, H, Dqk = q.shape
    scale = Dqk ** -0.5
    nblk = S // Q_BLOCK
    qb = q.reshape(B, nblk, Q_BLOCK, H, Dqk).transpose(1, 0, 2, 3, 4)
    kpos = jnp.arange(S)

    def block(args):
        qi, bi = args
        s = jnp.einsum('bqhd,bkhd->bhqk', qi, k, preferred_element_type=jnp.float32) * scale
        qpos = bi * Q_BLOCK + jnp.arange(Q_BLOCK)
        s = jnp.where(kpos[None, :] <= qpos[:, None], s, -jnp.inf)
        p = jax.nn.softmax(s, axis=-1).astype(v.dtype)
        return jnp.einsum('bhqk,bkhd->bqhd', p, v)

    out = lax.map(block, (qb, jnp.arange(nblk)))
    return out.transpose(1, 0, 2, 3, 4).reshape(B, S, H, v.shape[-1])


def dilated_window_attention(q, k, v, rel_bias, window, dil):
    B, S, H, D = q.shape
    n = window // dil
    span = dil * n
    Sp = -(-S // span) * span
    L = Sp // dil
    nc = L // n
    scale = D ** -0.5

    def to_strided(t):
        t = jnp.pad(t, ((0, 0), (0, Sp - S), (0, 0), (0, 0)))
        t = t.reshape(B, L, dil, H, D)
        t = t.transpose(0, 2, 1, 3, 4)
        return t.reshape(B, dil, nc, n, H, D)

    qq, kc, vc = to_strided(q), to_strided(k), to_strided(v)

    def with_prev(t):
        prev = jnp.concatenate([jnp.zeros_like(t[:, :, :1]), t[:, :, :-1]], axis=2)
        return jnp.concatenate([prev, t], axis=3)

    kk, vv = with_prev(kc), with_prev(vc)

    i = jnp.arange(n)[:, None]
    m = jnp.arange(2 * n)[None, :]
    rel = n + i - m
    band = (rel >= 0) & (rel <= n)
    c = jnp.arange(nc)[:, None, None]
    mask = (band[None] & ((c > 0) | (m[None] >= n)))[:, None]
    bias = rel_bias.astype(jnp.float32)[t5_bucket(rel * dil)].transpose(2, 0, 1)

    s = jnp.einsum('brcqhd,brckhd->brchqk', qq, kk, preferred_element_type=jnp.float32) * scale + bias
    s = jnp.where(mask, s, -jnp.inf)
    s_max = jnp.max(s, axis=-1, keepdims=True)
    e = jnp.exp(s - s_max)
    den = jnp.sum(e, axis=-1)
    o = jnp.einsum('brchqk,brckhd->brcqhd', e, vv.astype(jnp.float32))
    den_q = den.transpose(0, 1, 2, 4, 3)
    o = o / den_q[..., None]
    lse = s_max[..., 0].transpose(0, 1, 2, 4, 3) + jnp.log(den_q)

    def from_strided(t):
        t = t.reshape((B, dil, L) + t.shape[4:])
        t = jnp.moveaxis(t, 1, 2)
        return t.reshape((B, Sp) + t.shape[3:])[:, :S]

    return from_strided(o), from_strided(lse)


def hybrid_mixer(h, positions, rel_bias, w_in, q_norm_g, w_uq, kv_norm_g, w_ukv, w_out):
    B, S, _ = h.shape
    z = h @ w_in
    o1 = Q_LORA_RANK
    o2 = o1 + KV_LORA_RANK
    o3 = o2 + MLA_ROPE_DIM
    o4 = o3 + DIL_WIDTH
    o5 = o4 + DIL_WIDTH
    c_q, c_kv, k_rope = z[..., :o1], z[..., o1:o2], z[..., o2:o3]
    q_d, k_d, v_d = z[..., o3:o4], z[..., o4:o5], z[..., o5:]

    q = (rms_norm(c_q, q_norm_g) @ w_uq).reshape(B, S, MLA_HEADS, MLA_QK_DIM)
    q = jnp.concatenate([q[..., :MLA_NOPE_DIM], rope(q[..., MLA_NOPE_DIM:], positions)], axis=-1)
    kv = (rms_norm(c_kv, kv_norm_g) @ w_ukv).reshape(B, S, MLA_HEADS, MLA_NOPE_DIM + MLA_V_DIM)
    k_nope, v_mla = kv[..., :MLA_NOPE_DIM], kv[..., MLA_NOPE_DIM:]
    k_pe = rope(k_rope[:, :, None, :], positions)
    k_pe = jnp.broadcast_to(k_pe, (B, S, MLA_HEADS, MLA_ROPE_DIM))
    k = jnp.concatenate([k_nope, k_pe], axis=-1)
    o_mla = causal_block_attention(q, k, v_mla).reshape(B, S, MLA_WIDTH)

    qd = q_d.reshape(B, S, DIL_HEADS, DIL_HEAD_DIM)
    kd = k_d.reshape(B, S, DIL_HEADS, DIL_HEAD_DIM)
    vd = v_d.reshape(B, S, DIL_HEADS, DIL_HEAD_DIM)
    outs, lses = [], []
    for window, dil in DIL_PATTERNS:
        o_p, lse_p = dilated_window_attention(qd, kd, vd, rel_bias, window, dil)
        outs.append(o_p)
        lses.append(lse_p)
    w = jax.nn.softmax(jnp.stack(lses, axis=0), axis=0)
    o_dil = jnp.sum(w[..., None] * jnp.stack(outs, axis=0), axis=0)
    o_dil = o_dil.astype(h.dtype).reshape(B, S, DIL_WIDTH)

    return jnp.concatenate([o_mla, o_dil], axis=-1) @ w_out


def memory_cross_attention(h, mem_n, w_xq, w_xk, w_xv, w_xo):
    B, S, _ = h.shape
    M = mem_n.shape[1]
    q = (h @ w_xq).reshape(B, S, X_HEADS, X_HEAD_DIM)
    k = (mem_n @ w_xk).reshape(B, M, X_HEADS, X_HEAD_DIM)
    v = (mem_n @ w_xv).reshape(B, M, X_HEADS, X_HEAD_DIM)
    s = jnp.einsum('bshd,bmhd->bhsm', q, k, preferred_element_type=jnp.float32) * (X_HEAD_DIM ** -0.5)
    p = jax.nn.softmax(s, axis=-1).astype(v.dtype)
    o = jnp.einsum('bhsm,bmhd->bshd', p, v).reshape(B, S, D_MODEL)
    return o @ w_xo


def setup_inputs(seed: int = 0) -> dict:
    key = jax.random.key(seed)
    ks = jax.random.split(key, 32)

    def w(k, shape, fan_in):
        return jax.random.normal(k, shape, jnp.float32) * fan_in ** -0.5

    def gain(k, dim):
        return 1.0 + 0.05 * jax.random.normal(k, (DEPTH, dim), jnp.float32)

    offset = jax.random.randint(ks[2], (BATCH, 1), 0, 1024, dtype=jnp.int32)
    positions = offset + jnp.arange(SEQ, dtype=jnp.int32)[None, :]
    return {
        "x": jax.random.normal(ks[0], (BATCH, SEQ, D_MODEL), jnp.float32),
        "mem": jax.random.normal(ks[1], (BATCH, MEM_LEN, D_MODEL), jnp.float32),
        "positions": positions,
        "rel_bias": 0.5 * jax.random.normal(ks[3], (N_BUCKETS, DIL_HEADS), jnp.float32),
        "w_in": w(ks[4], (DEPTH, D_MODEL, IN_WIDTH), D_MODEL),
        "q_norm_g": gain(ks[5], Q_LORA_RANK),
        "w_uq": w(ks[6], (DEPTH, Q_LORA_RANK, MLA_HEADS * MLA_QK_DIM), Q_LORA_RANK),
        "kv_norm_g": gain(ks[7], KV_LORA_RANK),
        "w_ukv": w(ks[8], (DEPTH, KV_LORA_RANK, MLA_HEADS * (MLA_NOPE_DIM + MLA_V_DIM)), KV_LORA_RANK),
        "w_mix_out": w(ks[9], (DEPTH, MIX_WIDTH, D_MODEL), MIX_WIDTH),
        "pre_mix_g": gain(ks[10], D_MODEL),
        "post_mix_g": gain(ks[11], D_MODEL),
        "w_xq": w(ks[12], (DEPTH, D_MODEL, D_MODEL), D_MODEL),
        "w_xk": w(ks[13], (DEPTH, D_MODEL, D_MODEL), D_MODEL),
        "w_xv": w(ks[14], (DEPTH, D_MODEL, D_MODEL), D_MODEL),
        "w_xo": w(ks[15], (DEPTH, D_MODEL, D_MODEL), D_MODEL),
        "mem_g": gain(ks[16], D_MODEL),
        "pre_xattn_g": gain(ks[17], D_MODEL),
        "post_xattn_g": gain(ks[18], D_MODEL),
        "w_up": w(ks[19], (DEPTH, D_MODEL, D_FF), D_MODEL),
        "w_down": w(ks[20], (DEPTH, D_FF, D_MODEL), D_FF),
        "pre_mlp_g": gain(ks[21], D_MODEL),
        "post_mlp_g": gain(ks[22], D_MODEL),
    }


def reference(x, mem, positions, rel_bias, w_in, q_norm_g, w_uq, kv_norm_g, w_ukv, w_mix_out,
              pre_mix_g, post_mix_g, w_xq, w_xk, w_xv, w_xo, mem_g, pre_xattn_g, post_xattn_g,
              w_up, w_down, pre_mlp_g, post_mlp_g):
    for l in range(DEPTH):
        a = hybrid_mixer(rms_norm(x, pre_mix_g[l]), positions, rel_bias, w_in[l], q_norm_g[l],
                         w_uq[l], kv_norm_g[l], w_ukv[l], w_mix_out[l])
        x = x + rms_norm(a, post_mix_g[l])
        c = memory_cross_attention(rms_norm(x, pre_xattn_g[l]), rms_norm(mem, mem_g[l]),
                                   w_xq[l], w_xk[l], w_xv[l], w_xo[l])
        x = x + rms_norm(c, post_xattn_g[l])
        hdn = jnp.square(jax.nn.relu(rms_norm(x, pre_mlp_g[l]) @ w_up[l]))
        x = x + rms_norm(hdn @ w_down[l], post_mlp_g[l])
    return x
```

```python
import numpy as np
import concourse.bass as bass
import concourse.mybir as mybir
from concourse.bass_utils import run_bass_kernel_spmd

F32 = mybir.dt.float32
BF16 = mybir.dt.bfloat16
I32 = mybir.dt.int32
ALU = mybir.AluOpType
AF = mybir.ActivationFunctionType
AX = mybir.AxisListType


class Prog:
    ENGS = ("pe", "act", "dve", "pool", "sp")

    def __init__(self, nc, same_engine_sync=("act", "dve", "pool")):
        self.nc = nc
        self.ops = []
        self.same_engine_sync = set(same_engine_sync)
        self.nbar = 0
        self.bar_at = []

    def op(self, eng, fn, reads=(), writes=(), chan=None, free=False):
        def _flat(xs):
            o = []
            for x in xs:
                if isinstance(x, list):
                    o.extend(x)
                else:
                    o.append(x)
            return o
        reads = _flat(reads)
        writes = _flat(writes)
        for a in list(reads):
            if isinstance(a, tuple) and a and a[0] == "pb" and a not in writes:
                writes.append(a)
        self.ops.append(dict(eng=eng, fn=fn, reads=tuple(reads), writes=tuple(writes), chan=chan, free=free,
                             bar=self.nbar))

    def barrier(self):
        self.nbar += 1
        self.bar_at.append(len(self.ops))

    def pe(self, fn, reads=(), writes=()):
        self.op("pe", fn, reads, writes)

    def act(self, fn, reads=(), writes=()):
        self.op("act", fn, reads, writes)

    def dve(self, fn, reads=(), writes=()):
        self.op("dve", fn, reads, writes)

    def pool(self, fn, reads=(), writes=()):
        self.op("pool", fn, reads, writes)

    def dma(self, eng, chan, fn, reads=(), writes=(), free=False):
        self.op(eng, fn, reads, writes, chan=chan, free=free)

    def emit(self, stack):
        nc = self.nc
        ops = self.ops
        n = len(ops)
        last_writer = {}
        readers = {}
        deps = [None] * n
        last_eng = {}
        last_chan = {}
        bar_deps = set()
        bar_ptr = 0
        for i, o in enumerate(ops):
            while bar_ptr < len(self.bar_at) and self.bar_at[bar_ptr] <= i:
                bar_deps = set(last_eng.values()) | set(last_chan.values())
                bar_ptr += 1
            if o["chan"] is None:
                last_eng[o["eng"]] = i
            else:
                last_chan[o["chan"]] = i
            d = set()
            if not o["free"]:
                d |= bar_deps
            for a in o["reads"]:
                w = last_writer.get(a)
                if w is not None:
                    d.add(w)
            for a in o["writes"]:
                w = last_writer.get(a)
                if w is not None:
                    d.add(w)
                for r in readers.get(a, ()):
                    d.add(r)
            d.discard(i)
            deps[i] = d
            for a in o["reads"]:
                readers.setdefault(a, []).append(i)
            for a in o["writes"]:
                last_writer[a] = i
                readers[a] = []
        has_dependents = [False] * n
        for i in range(n):
            o = ops[i]
            for j in deps[i]:
                p = ops[j]
                if p["chan"] is None and p["eng"] == o["eng"] and o["chan"] is None \
                        and p["eng"] not in self.same_engine_sync:
                    continue
                has_dependents[j] = True
        final_waits = []
        eng_count = {e: 0 for e in self.ENGS}
        chan_count = {}
        sig = [None] * n
        for i, o in enumerate(ops):
            if o["chan"] is not None:
                c = ("chan", o["chan"])
                chan_count[c] = chan_count.get(c, 0) + 16
                sig[i] = (c, chan_count[c])
            elif has_dependents[i]:
                c = ("eng", o["eng"])
                eng_count[o["eng"]] += 1
                sig[i] = (c, eng_count[o["eng"]])
        semkeys = [("eng", e) for e in self.ENGS if eng_count[e] > 0] + sorted(chan_count.keys(), key=str)
        sems = {}
        for k in semkeys:
            sems[k] = stack.enter_context(nc.semaphore("s" + str(len(sems))))
        self.n_sems = len(sems)
        per_eng = {e: [] for e in self.ENGS}
        for i, o in enumerate(ops):
            per_eng[o["eng"]].append(i)
        waits = [None] * n
        waited = {e: {} for e in self.ENGS}
        for i, o in enumerate(ops):
            e = o["eng"]
            need = {}
            for j in deps[i]:
                p = ops[j]
                if sig[j] is None:
                    continue
                if p["chan"] is None and o["chan"] is None and p["eng"] == e \
                        and e not in self.same_engine_sync:
                    continue
                k, v = sig[j]
                if need.get(k, 0) < v:
                    need[k] = v
            w = []
            for k, v in need.items():
                if waited[e].get(k, 0) >= v:
                    continue
                waited[e][k] = v
                w.append((k, v))
            waits[i] = w
        final = [(k, v) for k, v in chan_count.items()]
        self.n_waits = sum(len(w) for w in waits)

        block = stack.enter_context(nc.Block())
        engobj = {"pe": "tensor", "act": "scalar", "dve": "vector", "pool": "gpsimd", "sp": "sync"}

        def make(e):
            idxs = per_eng[e]

            def body(eng):
                for i in idxs:
                    o = ops[i]
                    for k, v in waits[i]:
                        eng.wait_ge(sems[k], v)
                    ins = o["fn"](eng)
                    if sig[i] is not None:
                        k, v = sig[i]
                        if o["chan"] is not None:
                            ins.then_inc(sems[k], 16)
                        else:
                            ins.then_inc(sems[k], 1)
                if e == "sp":
                    for k, v in final:
                        eng.wait_ge(sems[k], v)
                    for ee in self.ENGS:
                        if eng_count[ee] > 0:
                            eng.wait_ge(sems[("eng", ee)], eng_count[ee])
            return body

        for e in self.ENGS:
            if per_eng[e] or e == "sp":
                getattr(block, engobj[e])(make(e))


S = 2048
DM = 1024
NT = 16
NL = 2
NPIECE = 41
GC = 40
EPS = 1e-6
PI = float(np.pi)
TWO_PI = float(2.0 * np.pi)
DILS = (1, 4, 16)
NEG = -30000.0
RING = 4
SLOT = 4096


def _t5_bucket(d):
    d = np.maximum(d, 0)
    df = np.maximum(d.astype(np.float32), np.float32(1.0))
    large = 16 + (np.log(df / np.float32(16)) / np.float32(np.log(2048 / 16)) * np.float32(16)).astype(np.int32)
    large = np.minimum(large, 31)
    return np.where(d < 16, d, large)


def _std(w):
    K, N = w.shape
    return np.ascontiguousarray(w.reshape(K // 128, 128, N).transpose(1, 0, 2))


def _piece(a):
    flat = a.reshape(128, -1)
    out = np.zeros((128, SLOT), np.float32)
    out[:, : flat.shape[1]] = flat
    return out


def prep_shared(inp):
    pieces = []
    gcol = np.zeros((128, NL * GC), np.float32)
    grow = np.zeros((NL, 6, 128, DM), np.float32)
    for l in range(NL):
        w_in = np.asarray(inp["w_in"][l])
        cols = [w_in[:, 0:640]]
        kr = w_in[:, 640:672]
        krs = np.concatenate([kr[:, 16:32], kr[:, 0:16]], axis=1)
        cols.append(np.concatenate([kr] * 4, axis=1))
        cols.append(np.concatenate([krs] * 4, axis=1))
        cols.append(w_in[:, 672:1696])
        wA = np.concatenate(cols, axis=1)
        assert wA.shape[1] == 1920
        wA = _std(wA)
        for i in range(4):
            pieces.append(_piece(wA[:, :, i * 512:(i + 1) * 512]))
        pieces.append(_piece(_std(w_in[:, 1696:2208])))
        w_uq = np.asarray(inp["w_uq"][l]).reshape(384, 8, 96)
        nope = w_uq[:, :, 0:64].reshape(384, 512)
        rope = w_uq[:, :, 64:96]
        ropes = np.concatenate([rope[:, :, 16:32], rope[:, :, 0:16]], axis=2)
        uqA = np.concatenate([nope, rope.reshape(384, 256), ropes.reshape(384, 256)], axis=1)
        pieces.append(_piece(_std(uqA)))
        w_ukv = np.asarray(inp["w_ukv"][l]).reshape(256, 8, 128)
        ukvA = np.concatenate([w_ukv[:, :, 0:64].reshape(256, 512), w_ukv[:, :, 64:128].reshape(256, 512)], axis=1)
        pieces.append(_piece(_std(ukvA)))
        for name in ("w_mix_out", "w_xq", "w_xk", "w_xv", "w_xo"):
            w = _std(np.asarray(inp[name][l]))
            pieces.append(_piece(w[:, :, 0:512]))
            pieces.append(_piece(w[:, :, 512:1024]))
        w = _std(np.asarray(inp["w_up"][l]))
        for i in range(8):
            pieces.append(_piece(w[:, :, i * 512:(i + 1) * 512]))
        w = _std(np.asarray(inp["w_down"][l]))
        for g in range(8):
            for hf in range(2):
                pieces.append(_piece(w[:, g * 4:(g + 1) * 4, hf * 512:(hf + 1) * 512]))
        b = l * GC
        for off, name, nk in ((0, "pre_mix_g", 8), (8, "pre_xattn_g", 8), (16, "pre_mlp_g", 8), (24, "mem_g", 8),
                              (32, "q_norm_g", 3), (35, "kv_norm_g", 2)):
            gcol[:, b + off:b + off + nk] = np.asarray(inp[name][l]).reshape(nk, 128).T
        for i, name in enumerate(("post_mix_g", "post_xattn_g", "post_mlp_g", "pre_mix_g", "pre_xattn_g", "pre_mlp_g")):
            grow[l, i] = np.broadcast_to(np.asarray(inp[name][l])[None, :], (128, DM))
    wts = np.stack(pieces, axis=0)
    rb = np.asarray(inp["rel_bias"])
    ext = np.concatenate([rb, np.full((1, 8), NEG, np.float32)], axis=0)
    ki = np.arange(128)[:, None]
    qi = np.arange(128)[None, :]
    btab = np.zeros((128, 3, 4, 2, 2, 128), np.float32)
    for p, dil in enumerate(DILS):
        for kb in range(2):
            rel = (128 if kb == 0 else 0) + qi - ki
            valid = (rel >= 0) & (rel <= 128)
            idx = np.where(valid, _t5_bucket(rel * dil), 32)
            for h in range(8):
                btab[:, p, h // 2, kb, h % 2, :] = ext[idx, h]
    rconst = np.zeros((128, 4), np.float32)
    prow = np.arange(128)
    rconst[:, 0] = (np.float32(10000.0) ** (-(prow % 16).astype(np.float32) / np.float32(16))).astype(np.float32)
    rconst[:, 1] = PI / 2
    rconst[:, 2] = np.where((prow % 32) < 16, PI, 0.0)
    return dict(wts=wts, gcol=gcol, grow=grow, btab=btab.reshape(128, -1), rconst=rconst)


class Arena:
    def __init__(self, ap, nbytes):
        self.ap = ap
        self.nbytes = nbytes
        self.top = 0
        self.peak = 0

    def alloc(self, shape, dt):
        es = 2 if dt == BF16 else 4
        n = 1
        for s in shape[1:]:
            n *= s
        nb = (n * es + 63) // 64 * 64
        off = self.top
        self.top += nb
        self.peak = max(self.peak, self.top)
        assert self.top <= self.nbytes, f"arena overflow {self.top} > {self.nbytes}"
        ap = self.ap[:, off // 2:(off + n * es) // 2]
        if dt != BF16:
            ap = ap.bitcast(dt)
        if len(shape) == 3:
            ap = ap.rearrange("p (a b) -> p a b", b=shape[2])
        elif len(shape) == 4:
            ap = ap.rearrange("p (a b c) -> p a b c", b=shape[2], c=shape[3])
        elif len(shape) == 5:
            ap = ap.rearrange("p (a b c d) -> p a b c d", b=shape[2], c=shape[3], d=shape[4])
        if shape[0] != 128:
            ap = ap[0:shape[0]]
        return ap

    def mark(self):
        return self.top

    def release(self, m):
        self.top = m


def build_program(n_sub=6, dumps=(), stop_at=99):
    from contextlib import ExitStack
    nc = bass.Bass("TRN2", target_bir_lowering=False)
    x_in = nc.dram_tensor("x", [NT, 128, DM], F32, kind="ExternalInput").ap()
    mem_in = nc.dram_tensor("mem", [2, 128, DM], F32, kind="ExternalInput").ap()
    pos_in = nc.dram_tensor("pos", [128, S], I32, kind="ExternalInput").ap()
    wts = nc.dram_tensor("wts", [NL * NPIECE, 128, SLOT], F32, kind="ExternalInput").ap()
    gcol_in = nc.dram_tensor("gcol", [128, NL * GC], F32, kind="ExternalInput").ap()
    grow_in = nc.dram_tensor("grow", [NL, 6, 128, DM], F32, kind="ExternalInput").ap()
    btab_in = nc.dram_tensor("btab", [128, 3 * 4 * 2 * 2 * 128], F32, kind="ExternalInput").ap()
    rconst_in = nc.dram_tensor("rconst", [128, 4], F32, kind="ExternalInput").ap()
    out = nc.dram_tensor("out", [NT, 128, DM], F32, kind="ExternalOutput").ap()
    xs = nc.dram_tensor("xs_scratch", [NT, 128, DM], F32).ap()
    dump_out = {}

    st = ExitStack()
    ARENA_BYTES = 206 * 1024
    arena_t = st.enter_context(nc.sbuf_tensor("arena", [128, ARENA_BYTES // 2], BF16))
    AR = Arena(arena_t[:, :], ARENA_BYTES)
    PS = [st.enter_context(nc.psum_tensor(f"ps{i}", [128, 1024], F32)) for i in range(4)]
    P = Prog(nc, same_engine_sync=("act", "dve", "pool"))

    def bank(i):
        return PS[i // 2][:, (i % 2) * 512:(i % 2) * 512 + 512], ("pb", i)

    ident = AR.alloc([128, 128], F32)
    ones_bf = AR.alloc([128, 128], BF16)
    ident_bf = AR.alloc([128, 128], BF16)
    gcol = AR.alloc([128, NL * GC], F32)
    rconst = AR.alloc([128, 4], F32)
    ring = AR.alloc([128, RING, SLOT], BF16)
    junk = AR.alloc([128, DM], BF16)
    stat = AR.alloc([128, 64], F32)
    epst = AR.alloc([128, 1], F32)
    actT = AR.alloc([128, 8, S], BF16)
    TC = AR.alloc([64, S], BF16)
    TS = AR.alloc([64, S], BF16)
    phase_mark = AR.mark()

    uid = [0]

    def U(name):
        uid[0] += 1
        return (name, uid[0])

    P.pool(lambda e: e.memset(ident, 0.0), writes=["ident"])
    P.pool(lambda e: e.affine_select(out=ident, in_=ident, pattern=[[-1, 128]], compare_op=ALU.not_equal,
                                     fill=1.0, base=0, channel_multiplier=1), reads=["ident"], writes=["ident"])
    P.pool(lambda e: e.memset(ones_bf, 1.0), writes=["ones"])
    P.pool(lambda e: e.memset(epst, EPS), writes=["epst"])
    P.dve(lambda e: e.tensor_copy(out=ident_bf, in_=ident), reads=["ident"], writes=["identb"])
    P.dma("sp", "gcol", lambda e: e.dma_start(out=gcol, in_=gcol_in), writes=["gcol"])
    P.dma("sp", "rconst", lambda e: e.dma_start(out=rconst, in_=rconst_in), writes=["rconst"])

    piece_ctr = [0]

    prefetched = {}

    def prefetch(layer, idx, nelem):
        if layer < NL and (layer, idx) not in prefetched:
            prefetched[(layer, idx)] = load_piece(layer, idx, nelem)

    def load_piece(layer, idx, nelem):
        if (layer, idx) in prefetched:
            return prefetched.pop((layer, idx))
        k = piece_ctr[0]
        piece_ctr[0] += 1
        slot = k % RING
        res = ("ring", slot)
        src = wts[layer * NPIECE + idx, :, 0:nelem].rearrange("p (a b) -> p a b", b=1024)
        dst = ring[:, slot, 0:nelem].rearrange("p (a b) -> p a b", b=1024)
        P.dma("pool", ("ring", slot), lambda e: e.dma_start(out=dst, in_=src), writes=[res])
        return ring[:, slot, :], res

    def wview(slot_ap, nk, ncol):
        return slot_ap[:, 0:nk * ncol].rearrange("p (a b) -> p a b", b=ncol)

    grow_ctr = [0]

    def load_grow(gbuf, k, layer, which):
        res = ("grow", k)
        P.dma("sp", ("grow", k), lambda e: e.dma_start(out=gbuf[:, k, :], in_=grow_in[layer, which]), writes=[res])
        return gbuf[:, k, :], res

    def dump(name, ap, shape, dt, reads):
        t = nc.dram_tensor("dbg_" + name, list(shape), dt, kind="ExternalOutput").ap()
        dump_out[name] = t
        P.dma("sp", ("dump", name), lambda e: e.dma_start(out=t, in_=ap), reads=reads)

    def rstd_from_ss(ss_ap, res, n, scale):
        P.act(lambda e: e.activation(out=ss_ap, in_=ss_ap, func=AF.Ln, scale=scale, bias=epst[:, 0:1]), reads=[res, "epst"], writes=[res])
        P.act(lambda e: e.activation(out=ss_ap, in_=ss_ap, func=AF.Exp, scale=-0.5), reads=[res], writes=[res])

    evac_flip = [0]

    def evac_scaled(out_ap, in_ap, scale_ap, reads, writes):
        evac_flip[0] ^= 1
        if evac_flip[0]:
            if scale_ap is None:
                P.act(lambda e: e.copy(out=out_ap, in_=in_ap), reads=reads, writes=writes)
            else:
                P.act(lambda e: e.activation(out=out_ap, in_=in_ap, func=AF.Copy, scale=scale_ap), reads=reads, writes=writes)
        else:
            if scale_ap is None:
                P.dve(lambda e: e.tensor_copy(out=out_ap, in_=in_ap), reads=reads, writes=writes)
            else:
                P.dve(lambda e: e.tensor_scalar(out=out_ap, in0=in_ap, scalar1=scale_ap, scalar2=None, op0=ALU.mult),
                      reads=reads, writes=writes)

    def phase_norm(src_tile, src_res, ntiles, gbase, dstT, dst_res_fn, width_tiles=4):
        m = AR.mark()
        xt = AR.alloc([128, 4, DM], F32)
        xb = AR.alloc([128, 4, DM], BF16)
        PSbf = [PS[0][:, 0:512].bitcast(BF16), PS[0][:, 512:1024].bitcast(BF16)]
        ssr = ("stat", "norm")
        ngroups = (ntiles + width_tiles - 1) // width_tiles
        for g in range(ngroups):
            tts = list(range(g * width_tiles, min(ntiles, (g + 1) * width_tiles)))
            for q, tt in enumerate(tts):
                P.dma("sp", ("xt", q), lambda e, q=q, tt=tt: e.dma_start(out=xt[:, q, :], in_=src_tile(tt)),
                      reads=[src_res(tt)], writes=[("xt", q)])
                P.act(lambda e, q=q: e.activation(out=junk, in_=xt[:, q, :], func=AF.Square, accum_out=stat[:, q:q + 1]),
                      reads=[("xt", q)], writes=["junk", (ssr, q)])
            nq = len(tts)
            sres = [(ssr, q) for q in range(nq)]
            P.dve(lambda e, nq=nq: e.tensor_scalar(out=stat[:, 0:nq], in0=stat[:, 0:nq], scalar1=1.0 / DM, scalar2=EPS,
                                                   op0=ALU.mult, op1=ALU.add), reads=sres, writes=sres)
            P.act(lambda e, nq=nq: e.activation(out=stat[:, 0:nq], in_=stat[:, 0:nq], func=AF.Ln), reads=sres, writes=sres)
            P.act(lambda e, nq=nq: e.activation(out=stat[:, 0:nq], in_=stat[:, 0:nq], func=AF.Exp, scale=-0.5), reads=sres, writes=sres)
            for q in range(nq):
                P.dve(lambda e, q=q: e.tensor_scalar(out=xb[:, q, :], in0=xt[:, q, :], scalar1=stat[:, q:q + 1], scalar2=None,
                                                     op0=ALU.mult), reads=[("xt", q), (ssr, q)], writes=[("xb", q)])
            for j in range(8):
                pbf = PSbf[j % 2]
                pres = ("pb", j % 2)
                for q in range(nq):
                    P.pe(lambda e, q=q, j=j, pbf=pbf: e.transpose(out=pbf[:, q * 128:(q + 1) * 128], in_=xb[:, q, j * 128:(j + 1) * 128],
                                                                 identity=ident_bf), reads=[("xb", q), "identb"], writes=[pres])
                t0 = tts[0] * 128
                evac_scaled(dstT[:, j, t0:t0 + nq * 128], pbf[:, 0:nq * 128], gcol[:, gbase + j:gbase + j + 1],
                            reads=[pres, "gcol"], writes=[dst_res_fn(j, g)])
        AR.release(m)
        P.barrier()

    def phase_out(featT, feat_res_fn, nk, layer, piece_idx, grow_which, src_tile, src_res, dst_tile, dst_res, acc=None,
                  next_norm=None, pre=(), deferred=False):
        m = AR.mark()
        if not deferred:
            P.barrier()
        ADDSPLIT = 640
        ynres = lambda q: [("yn", q, 0), ("yn", q, 1)]
        xt4 = AR.alloc([128, 4, DM], F32)
        xt = xt4[:, 0:2, :]
        yn = xt4[:, 2:4, :]
        gbuf = AR.alloc([128, 2, DM], F32)
        g_ap, g_res = load_grow(gbuf, 0, layer, grow_which)
        if next_norm is not None:
            xbn = AR.alloc([128, 3, DM], BF16)
            g2_ap, g2_res = load_grow(gbuf, 1, next_norm[0], next_norm[1])
            PSbf = [PS[0][:, 0:512].bitcast(BF16), PS[0][:, 512:1024].bitcast(BF16)]
        if acc is None:
            w0, w0r = load_piece(layer, piece_idx, 8 * 512)
            w1, w1r = load_piece(layer, piece_idx + 1, 8 * 512)
            wv = [wview(w0, 8, 512), wview(w1, 8, 512)]
            wr = [w0r, w1r]
        for (pl, pi, pn) in pre:
            prefetch(pl, pi, pn)

        def tail_front(tt):
            q = tt % 2
            q3 = tt % 3
            s2 = ("stat", "out2", q)
            P.act(lambda e: e.activation(out=junk, in_=yn[:, q, :], func=AF.Square, accum_out=stat[:, 10 + q:11 + q]),
                  reads=ynres(q), writes=["junk", s2])
            rstd_from_ss(stat[:, 10 + q:11 + q], s2, 1, 1.0 / DM)
            P.dve(lambda e: e.scalar_tensor_tensor(out=xbn[:, q3, :], in0=yn[:, q, :], scalar=stat[:, 10 + q:11 + q], in1=g2_ap,
                                                   op0=ALU.mult, op1=ALU.mult),
                  reads=ynres(q) + [s2, g2_res], writes=[("xbn", q3)])

        def tail_pe(tt):
            q = tt % 3
            for half in range(2):
                for jj in range(4):
                    j = half * 4 + jj
                    P.pe(lambda e, j=j, jj=jj, half=half: e.transpose(out=PSbf[half][:, jj * 128:(jj + 1) * 128],
                                                                        in_=xbn[:, q, j * 128:(j + 1) * 128], identity=ident_bf),
                         reads=[("xbn", q), "identb"], writes=[("pb", half)])
                outv = actT[:, half * 4:half * 4 + 4, tt * 128:(tt + 1) * 128]
                inv = PSbf[half][:, 0:512].rearrange("p (a b) -> p a b", a=4)
                wres_ = [("actT", half * 4 + jj, tt) for jj in range(4)]
                if half == 0:
                    P.act(lambda e, outv=outv, inv=inv: e.copy(out=outv, in_=inv), reads=[("pb", half)], writes=wres_)
                else:
                    P.dve(lambda e, outv=outv, inv=inv: e.tensor_copy(out=outv, in_=inv), reads=[("pb", half)], writes=wres_)

        def tile(tt):
            q = tt % 2
            P.dma("sp", ("xt", q), lambda e, q=q, tt=tt: e.dma_start(out=xt[:, q, :], in_=src_tile(tt)),
                  reads=[src_res(tt)], writes=[("xt", q)])
            if acc is None:
                Y = PS[2 + q]
                yres = [("pb", 4 + 2 * q), ("pb", 5 + 2 * q)]
                for hf in range(2):
                    for k in range(nk):
                        P.pe(lambda e, hf=hf, k=k, tt=tt, Y=Y: e.matmul(Y[:, hf * 512:(hf + 1) * 512], lhsT=featT[:, k, tt * 128:(tt + 1) * 128],
                                                                        rhs=wv[hf][:, k, :], start=(k == 0), stop=(k == nk - 1)),
                             reads=[feat_res_fn(k, tt), wr[hf]], writes=[yres[hf]])
                y_ap = Y[:, :]
            else:
                y_ap = acc[0][:, tt, :]
                yres = acc[1](tt)
            if next_norm is not None and tt >= 3:
                tail_pe(tt - 3)
            sres = ("stat", "out", q)
            P.act(lambda e, q=q, y_ap=y_ap: e.activation(out=junk, in_=y_ap, func=AF.Square, accum_out=stat[:, 8 + q:9 + q]),
                  reads=yres, writes=["junk", sres])
            rstd_from_ss(stat[:, 8 + q:9 + q], sres, 1, 1.0 / DM)
            P.dve(lambda e, q=q, y_ap=y_ap: e.scalar_tensor_tensor(out=yn[:, q, :], in0=y_ap, scalar=stat[:, 8 + q:9 + q], in1=g_ap,
                                                                   op0=ALU.mult, op1=ALU.mult),
                  reads=yres + [sres, g_res], writes=ynres(q))
            P.dve(lambda e, q=q: e.tensor_tensor(out=yn[:, q, 0:ADDSPLIT], in0=yn[:, q, 0:ADDSPLIT], in1=xt[:, q, 0:ADDSPLIT], op=ALU.add),
                  reads=[("yn", q, 0), ("xt", q)], writes=[("yn", q, 0)])
            P.pool(lambda e, q=q: e.tensor_tensor(out=yn[:, q, ADDSPLIT:DM], in0=yn[:, q, ADDSPLIT:DM], in1=xt[:, q, ADDSPLIT:DM], op=ALU.add),
                   reads=[("yn", q, 1), ("xt", q)], writes=[("yn", q, 1)])
            P.dma("pool", ("yst", q), lambda e, q=q, tt=tt: e.dma_start(out=dst_tile(tt), in_=yn[:, q, :]),
                  reads=ynres(q), writes=[dst_res(tt)])
            if next_norm is not None and tt >= 1:
                tail_front(tt - 1)
        def finish():
            if next_norm is not None:
                tail_front(NT - 1)
                tail_pe(NT - 3)
                tail_pe(NT - 2)
                tail_pe(NT - 1)
            AR.release(m)
            P.barrier()

        if deferred:
            return tile, finish
        for tt in range(NT):
            tile(tt)
        finish()

    def x_src(sub):
        if sub == 0:
            return (lambda tt: x_in[tt]), (lambda tt: ("xin", tt))
        return (lambda tt: xs[tt]), (lambda tt: ("xs", tt))

    def x_dst(sub):
        if sub == n_sub - 1:
            return (lambda tt: out[tt]), (lambda tt: ("xout", tt))
        return (lambda tt: xs[tt]), (lambda tt: ("xs", tt))

    def next_norm_of(sub):
        if sub + 1 >= n_sub:
            return None
        return ((sub + 1) // 3, 3 + (sub + 1) % 3)

    hT_res = lambda j, g: [("actT", j, 4 * g + t) for t in range(4)]

    def actT_tile_res(k, tt):
        return ("actT", k, tt)

    def mixer(layer, sub):
        srcT, srcR = x_src(sub)
        dstT, dstR = x_dst(sub)
        gb = layer * GC
        if sub == 0:
            phase_norm(srcT, srcR, NT, gb + 0, actT, hT_res)
        if stop_at <= 1:
            return
        m0 = AR.mark()
        craw = AR.alloc([128, 5, S], BF16)
        kpe = AR.alloc([64, S], BF16)
        m1d = AR.mark()
        qdT = AR.alloc([128, 4, S], BF16)
        kdT = AR.alloc([128, 4, S], BF16)
        vdT = AR.alloc([128, 4, S], BF16)
        m1 = AR.mark()
        if layer == 0:
            posi = AR.alloc([64, 512], I32)
            ang = AR.alloc([64, 512], F32)
            kint = AR.alloc([64, 512], I32)
            kf = AR.alloc([64, 512], F32)
        def rope_chunk(n):
            cs = slice(n * 512, (n + 1) * 512)
            P.dma("sp", "posi", lambda e, cs=cs: e.dma_start(out=posi, in_=pos_in[0:64, cs]), writes=["posi"])
            for T, col, tres in ((TC, 1, "TC"), (TS, 2, "TS")):
                P.dve(lambda e: e.tensor_copy(out=ang, in_=posi), reads=["posi"], writes=["ang"])
                P.dve(lambda e, col=col: e.tensor_scalar(out=ang, in0=ang, scalar1=rconst[0:64, 0:1], scalar2=rconst[0:64, col:col + 1],
                                                         op0=ALU.mult, op1=ALU.add), reads=["ang", "rconst"], writes=["ang"])
                P.dve(lambda e: e.tensor_scalar(out=kint, in0=ang, scalar1=1.0 / TWO_PI, scalar2=None, op0=ALU.mult),
                      reads=["ang"], writes=["kint"])
                P.dve(lambda e: e.tensor_copy(out=kf, in_=kint), reads=["kint"], writes=["kf"])
                P.dve(lambda e: e.scalar_tensor_tensor(out=ang, in0=kf, scalar=-TWO_PI, in1=ang, op0=ALU.mult, op1=ALU.add),
                      reads=["kf", "ang"], writes=["ang"])
                P.dve(lambda e: e.tensor_scalar(out=kf, in0=ang, scalar1=PI, scalar2=-TWO_PI, op0=ALU.is_gt, op1=ALU.mult),
                      reads=["ang"], writes=["kf"])
                P.dve(lambda e: e.tensor_tensor(out=ang, in0=ang, in1=kf, op=ALU.add), reads=["ang", "kf"], writes=["ang"])
                P.dve(lambda e: e.tensor_scalar(out=ang, in0=ang, scalar1=3.1415925, scalar2=-3.1415925, op0=ALU.min, op1=ALU.max),
                      reads=["ang"], writes=["ang"])
                P.act(lambda e, T=T, cs=cs: e.activation(out=T[:, cs], in_=ang, func=AF.Sin), reads=["ang"], writes=[(tres, n)])

        if stop_at <= 1.5:
            return
        sqb = AR.alloc([128, 5, 512], BF16)
        rqb = AR.alloc([128, 2, 512], F32)
        tmpA = AR.alloc([64, 512], F32)
        tmpB = AR.alloc([64, 512], F32)
        wA = []
        for i in range(4):
            w, r = load_piece(layer, i, 8 * 512 if i < 3 else 8 * 384)
            wA.append((wview(w, 8, 512 if i < 3 else 384), r))

        def wA_chunk(c):
            w, r = wA[c // 4]
            return (lambda j: w[:, j, (c % 4) * 128:(c % 4) * 128 + 128]), r

        if stop_at <= 1.7:
            return
        bank_ctr = [0]

        def next_bank():
            b = 2 + (bank_ctr[0] % 6)
            bank_ctr[0] += 1
            return bank(b)

        def proj_chunk(c, n):
            pb, pres = next_bank()
            lw, wres = wA_chunk(c)
            for j in range(8):
                P.pe(lambda e, j=j, pb=pb, lw=lw: e.matmul(pb, lhsT=lw(j), rhs=actT[:, j, n * 512:(n + 1) * 512],
                                                        start=(j == 0), stop=(j == 7)),
                     reads=[hT_res(j, n), wres], writes=[pres])
            return pb, pres

        for n in range(4):
            if layer == 0:
                rope_chunk(n)
            cs = slice(n * 512, (n + 1) * 512)
            for c in range(5):
                pb, pres = proj_chunk(c, n)
                if stop_at <= 1.75:
                    return
                gc = gb + 32 + c
                P.act(lambda e, c=c, pb=pb: e.activation(out=sqb[:, c, :], in_=pb, func=AF.Square), reads=[pres], writes=[("sqb", c)])
                P.dve(lambda e, c=c, pb=pb, gc=gc, cs=cs: e.tensor_scalar(out=craw[:, c, cs], in0=pb, scalar1=gcol[:, gc:gc + 1], scalar2=None,
                                                                        op0=ALU.mult), reads=[pres, "gcol"], writes=[("craw", c, n)])
                if stop_at <= 1.8:
                    return
            for which, (c0, c1, dim) in enumerate(((0, 3, 384), (3, 5, 256))):
                pb, pres = next_bank()
                for c in range(c0, c1):
                    P.pe(lambda e, c=c, pb=pb, c0=c0, c1=c1: e.matmul(pb, lhsT=ones_bf, rhs=sqb[:, c, :], start=(c == c0), stop=(c == c1 - 1)),
                         reads=["ones", ("sqb", c)], writes=[pres])
                rres = ("rqb", which)
                P.dve(lambda e, pb=pb, which=which, dim=dim: e.tensor_scalar(out=rqb[:, which, :], in0=pb, scalar1=1.0 / dim, scalar2=EPS,
                                                                           op0=ALU.mult, op1=ALU.add), reads=[pres], writes=[rres])
                P.act(lambda e, which=which: e.activation(out=rqb[:, which, :], in_=rqb[:, which, :], func=AF.Ln), reads=[rres], writes=[rres])
                P.act(lambda e, which=which: e.activation(out=rqb[:, which, :], in_=rqb[:, which, :], func=AF.Exp, scale=-0.5),
                      reads=[rres], writes=[rres])
                for c in range(c0, c1):
                    P.dve(lambda e, c=c, which=which, cs=cs: e.tensor_tensor(out=craw[:, c, cs], in0=craw[:, c, cs], in1=rqb[:, which, :],
                                                                           op=ALU.mult), reads=[("craw", c, n), rres], writes=[("craw", c, n)])
            if stop_at <= 1.9:
                return
            pa, pares = proj_chunk(5, n)
            pbb, pbres = proj_chunk(6, n)
            P.dve(lambda e, pa=pa, cs=cs: e.tensor_tensor(out=tmpA, in0=pa[0:64, :], in1=TC[:, cs], op=ALU.mult),
                  reads=[pares, ("TC", n)], writes=["tmpA"])
            P.dve(lambda e, pbb=pbb, cs=cs: e.tensor_tensor(out=tmpB, in0=pbb[0:64, :], in1=TS[:, cs], op=ALU.mult),
                  reads=[pbres, ("TS", n)], writes=["tmpB"])
            P.dve(lambda e, cs=cs: e.tensor_tensor(out=kpe[:, cs], in0=tmpA, in1=tmpB, op=ALU.add),
                  reads=["tmpA", "tmpB"], writes=[("kpe", n)])
            if stop_at <= 1.95:
                return
            for i in range(4):
                pb, pres = proj_chunk(7 + i, n)
                evac_scaled(qdT[:, i, cs], pb, None, reads=[pres], writes=[("qdT", i, n)])
                pb, pres = proj_chunk(11 + i, n)
                evac_scaled(kdT[:, i, cs], pb, None, reads=[pres], writes=[("kdT", i, n)])
        wV, wVr = load_piece(layer, 4, 8 * 512)
        wVv = wview(wV, 8, 512)
        for i in range(4):
            for n in range(4):
                cs = slice(n * 512, (n + 1) * 512)
                pb, pres = next_bank()
                for j in range(8):
                    P.pe(lambda e, j=j, pb=pb, i=i, cs=cs: e.matmul(pb, lhsT=wVv[:, j, i * 128:(i + 1) * 128], rhs=actT[:, j, cs],
                                                                  start=(j == 0), stop=(j == 7)),
                         reads=[hT_res(j, n), wVr], writes=[pres])
                evac_scaled(vdT[:, i, cs], pb, None, reads=[pres], writes=[("vdT", i, n)])
        prefetch(layer, 5, 3 * 1024)
        prefetch(layer, 6, 2 * 1024)
        prefetch(layer, 7, 8 * 512)
        prefetch(layer, 8, 8 * 512)
        AR.release(m1)
        if "rqb" in dumps:
            dump("rqb", rqb, [128, 2, 512], F32, [("rqb", 0), ("rqb", 1)])
            dump("sqb", sqb, [128, 5, 512], BF16, [("sqb", c) for c in range(5)])
        if "craw" in dumps:
            dump("craw", craw, [128, 5, S], BF16, [("craw", c, n) for c in range(5) for n in range(4)])
            dump("kpe", kpe, [64, S], BF16, [("kpe", n) for n in range(4)])
            dump("qdT", qdT, [128, 4, S], BF16, [("qdT", i, n) for i in range(4) for n in range(4)])
            dump("vdT", vdT, [128, 4, S], BF16, [("vdT", i, n) for i in range(4) for n in range(4)])

        if stop_at <= 2:
            return
        P.barrier()
        m2 = AR.mark()
        btab = AR.alloc([128, 3, 4, 512], BF16)
        P.dma("pool", "btab", lambda e: e.dma_start(out=btab.rearrange("p a b c -> p (a b c)").rearrange("p (a b) -> p a b", b=1024),
                                                  in_=btab_in.rearrange("p (a b) -> p a b", b=1024)), writes=["btab"])
        P.dve(lambda e: e.tensor_scalar(out=btab, in0=btab, scalar1=8.0, scalar2=None, op0=ALU.mult), reads=["btab"], writes=["btab"])
        ND = AR.alloc([128, 2, S], F32)
        vblk = AR.alloc([128, 4, 256], BF16)
        pT = AR.alloc([128, 3, 512], BF16)
        qpad = AR.alloc([128, 2, 2, S], BF16)
        P.pool(lambda e: e.memset(qpad, 0.0), writes=[("qpad", 0), ("qpad", 1)])
        recj = junk.bitcast(F32)[0:64, :]
        P.pool(lambda e: e.memset(vblk, 1.0), writes=[("vblk", s) for s in range(4)])
        PSb = [PS[0][:, 0:512].bitcast(BF16), PS[0][:, 512:1024].bitcast(BF16)]
        def dil_blocks(i):
            blks = []
            for p, dil in enumerate(DILS):
                ncb = 16 // dil
                for r in range(dil):
                    for c in range(ncb):
                        blks.append((p, dil, r, c))
            return blks

        gctr = [0]

        def front(i, blk, st_):
            p, dil, r, c = blk
            g_ = gctr[0]
            gctr[0] += 1
            t0 = r + dil * 128 * c
            toks = slice(t0, t0 + dil * 127 + 1, dil)
            if dil == 1:
                chunks_touched = [c // 4]
            elif dil == 4:
                chunks_touched = [c]
            else:
                chunks_touched = [0, 1, 2, 3]
            slot = g_ % 4
            par = g_ % 2
            s3 = g_ % 3
            P.pe(lambda e: e.transpose(out=PSb[par][:, 0:128], in_=vdT[:, i, toks], identity=ident_bf),
                 reads=[("vdT", i, n) for n in chunks_touched] + ["identb"], writes=[("pb", par)])
            if g_ % 2 == 0:
                P.act(lambda e: e.copy(out=vblk[:, slot, :].rearrange("p (h x) -> p h x", h=2)[:, :, 0:64],
                                       in_=PSb[par][:, 0:128].rearrange("p (h x) -> p h x", h=2)),
                      reads=[("pb", par)], writes=[("vblk", slot)])
            else:
                P.dve(lambda e: e.tensor_copy(out=vblk[:, slot, :].rearrange("p (h x) -> p h x", h=2)[:, :, 0:64],
                                              in_=PSb[par][:, 0:128].rearrange("p (h x) -> p h x", h=2)),
                      reads=[("pb", par)], writes=[("vblk", slot)])
            Sb, Sres = bank(2 + s3)
            kbs = [1] if c == 0 else [0, 1]
            for kb in kbs:
                kt0 = t0 if kb == 1 else t0 - dil * 128
                ktoks = slice(kt0, kt0 + dil * 127 + 1, dil)
                P.pe(lambda e, ktoks=ktoks, kb=kb: e.matmul(Sb[:, kb * 256:kb * 256 + 256], lhsT=kdT[:, i, ktoks], rhs=qpad[:, i % 2, :, toks],
                                                           start=True, stop=False),
                     reads=[("kdT", i, n) for n in range(4)] + [("qpad", i % 2)], writes=[Sres])
                P.pe(lambda e, kb=kb: e.matmul(Sb[:, kb * 256:kb * 256 + 256], lhsT=ident_bf, rhs=btab[:, p, i, kb * 256:kb * 256 + 256],
                                              start=False, stop=True),
                     reads=["identb", "btab"], writes=[Sres])
            lo = 256 if c == 0 else 0
            P.act(lambda e: e.activation(out=pT[:, s3, lo:512], in_=Sb[:, lo:512], func=AF.Exp, scale=0.125),
                  reads=[Sres], writes=[("pT", s3)])
            st_.update(dict(toks=toks, chunks=chunks_touched, slot=slot, s3=s3, kbs=kbs, p=p, c=c))

        def back(i, st_, prev_slot):
            s3 = st_["s3"]
            kbs = st_["kbs"]
            slot = st_["slot"]
            Ob, Ores = bank(5 + s3)
            for hh in range(2):
                for kb in kbs:
                    vs = slot if kb == 1 else prev_slot
                    col = (kb * 2 + hh) * 128
                    P.pe(lambda e, hh=hh, vs=vs, col=col, kb=kb: e.matmul(
                            Ob[:, hh * 128:(hh + 1) * 128], lhsT=vblk[:, vs, hh * 128:(hh + 1) * 128], rhs=pT[:, s3, col:col + 128],
                            start=(kb == kbs[0]), stop=(kb == 1)),
                         reads=[("vblk", vs), ("pT", s3)], writes=[Ores])
            ndv = ND[:, :, st_["toks"]]
            obv = Ob[:, 0:256].rearrange("p (h x) -> p h x", h=2)
            nd_res = [("ND", n) for n in st_["chunks"]]
            if st_["p"] == 0:
                P.dve(lambda e: e.tensor_copy(out=ndv, in_=obv), reads=[Ores], writes=nd_res)
            else:
                P.dve(lambda e: e.tensor_tensor(out=ndv, in0=ndv, in1=obv, op=ALU.add), reads=[Ores] + nd_res, writes=nd_res)

        DDEPTH = 2

        def load_qpad(i):
            bq = i % 2
            P.dve(lambda e: e.tensor_copy(out=qpad[0:64, bq, 0, :], in_=qdT[0:64, i, :]),
                  reads=[("qdT", i, n) for n in range(4)], writes=[("qpad", bq)])
            P.act(lambda e: e.copy(out=qpad[64:128, bq, 1, :], in_=qdT[64:128, i, :]),
                  reads=[("qdT", i, n) for n in range(4)], writes=[("qpad", bq)])

        def norm_chunk(i, n):
            cs = slice(n * 512, (n + 1) * 512)
            for hh in range(2):
                P.act(lambda e, hh=hh: e.activation(out=recj, in_=ND[64:128, hh, cs], func=AF.Ln), reads=[("ND", n)], writes=["junk"])
                P.act(lambda e: e.activation(out=recj, in_=recj, func=AF.Exp, scale=-1.0), reads=["junk"], writes=["junk"])
                P.dve(lambda e, hh=hh: e.tensor_tensor(out=actT[hh * 64:hh * 64 + 64, 4 + i, cs], in0=ND[0:64, hh, cs], in1=recj, op=ALU.mult),
                      reads=[("ND", n), "junk"], writes=[hT_res(4 + i, n)])

        load_qpad(0)
        for i in range(4):
            blks = dil_blocks(i)
            states = [dict() for _ in blks]
            for bi in range(min(DDEPTH, len(blks))):
                front(i, blks[bi], states[bi])
            for bi in range(len(blks)):
                if bi + DDEPTH < len(blks):
                    front(i, blks[bi + DDEPTH], states[bi + DDEPTH])
                if i > 0 and bi < 16 and bi % 4 == 0:
                    norm_chunk(i - 1, bi // 4)
                if bi == 24 and i + 1 < 4:
                    load_qpad(i + 1)
                prev_slot = states[bi - 1]["slot"] if states[bi]["c"] > 0 else None
                back(i, states[bi], prev_slot)
        for n in range(4):
            norm_chunk(3, n)
        AR.release(m2)

        if stop_at <= 3:
            return
        AR.release(m1d)
        P.barrier()
        wq, wqr = load_piece(layer, 5, 3 * 1024)
        wk, wkr = load_piece(layer, 6, 2 * 1024)
        wqv = wview(wq, 3, 1024)
        wkv = wview(wk, 2, 1024)
        q96 = AR.alloc([128, 2, S], BF16)
        k96 = AR.alloc([128, 2, S], BF16)
        vp = AR.alloc([128, NT, 8, 128], BF16)
        pTm = AR.alloc([128, 6, 512], BF16)
        recm = AR.alloc([64, 2, 512], F32)
        tA = AR.alloc([64, 512], F32)
        tB = AR.alloc([64, 512], F32)
        P.pool(lambda e: e.memset(vp, 1.0), writes=[("vp", tt) for tt in range(NT)])
        for hh in range(2):
            for n in range(4):
                cs = slice(n * 512, (n + 1) * 512)
                P.dve(lambda e, hh=hh, cs=cs: e.tensor_copy(out=k96[64:96, hh, cs], in_=kpe[0:32, cs]),
                      reads=[("kpe", n)], writes=[("k96", hh, n)])
        for tt in range(NT):
            pb, pres = next_bank()
            for c in range(2):
                P.pe(lambda e, c=c, pb=pb, tt=tt: e.matmul(pb, lhsT=craw[:, 3 + c, tt * 128:(tt + 1) * 128], rhs=wkv[:, c, 512:1024],
                                                         start=(c == 0), stop=(c == 1)),
                     reads=[("craw", 3 + c, tt // 4), wkr], writes=[pres])
            evac_scaled(vp[:, tt, :, 0:64], pb.rearrange("p (h x) -> p h x", h=8), None, reads=[pres], writes=[("vp", tt)])
        scale_m = float(96 ** -0.5)
        pctr = [0]
        for i in range(4):
            for n in range(4):
                cs = slice(n * 512, (n + 1) * 512)
                pb, pres = next_bank()
                for c in range(3):
                    P.pe(lambda e, c=c, pb=pb, cs=cs, i=i: e.matmul(pb, lhsT=wqv[:, c, i * 128:(i + 1) * 128], rhs=craw[:, c, cs],
                                                                  start=(c == 0), stop=(c == 2)), reads=[("craw", c, n), wqr], writes=[pres])
                for hh in range(2):
                    evac_scaled(q96[0:64, hh, cs], pb[hh * 64:hh * 64 + 64, :], None, reads=[pres], writes=[("q96", hh, n)])
                pb, pres = next_bank()
                for c in range(2):
                    P.pe(lambda e, c=c, pb=pb, cs=cs, i=i: e.matmul(pb, lhsT=wkv[:, c, i * 128:(i + 1) * 128], rhs=craw[:, 3 + c, cs],
                                                                  start=(c == 0), stop=(c == 1)), reads=[("craw", 3 + c, n), wkr], writes=[pres])
                for hh in range(2):
                    evac_scaled(k96[0:64, hh, cs], pb[hh * 64:hh * 64 + 64, :], None, reads=[pres], writes=[("k96", hh, n)])
                pa, pares = next_bank()
                pbb, pbres = next_bank()
                for c in range(3):
                    P.pe(lambda e, c=c, pa=pa, cs=cs, i=i: e.matmul(pa[0:64, :], lhsT=wqv[:, c, 512 + i * 64:512 + i * 64 + 64], rhs=craw[:, c, cs],
                                                                  start=(c == 0), stop=(c == 2)), reads=[("craw", c, n), wqr], writes=[pares])
                for c in range(3):
                    P.pe(lambda e, c=c, pbb=pbb, cs=cs, i=i: e.matmul(pbb[0:64, :], lhsT=wqv[:, c, 768 + i * 64:768 + i * 64 + 64], rhs=craw[:, c, cs],
                                                                    start=(c == 0), stop=(c == 2)), reads=[("craw", c, n), wqr], writes=[pbres])
                P.dve(lambda e, pa=pa, cs=cs: e.tensor_tensor(out=tA, in0=pa[0:64, :], in1=TC[:, cs], op=ALU.mult),
                      reads=[pares, ("TC", n)], writes=["tA"])
                P.dve(lambda e, pbb=pbb, cs=cs: e.tensor_tensor(out=tB, in0=pbb[0:64, :], in1=TS[:, cs], op=ALU.mult),
                      reads=[pbres, ("TS", n)], writes=["tB"])
                for hh in range(2):
                    P.dve(lambda e, cs=cs, hh=hh: e.tensor_tensor(out=q96[64:96, hh, cs], in0=tA[hh * 32:hh * 32 + 32, :],
                                                                in1=tB[hh * 32:hh * 32 + 32, :], op=ALU.add),
                          reads=["tA", "tB"], writes=[("q96", hh, n)])
            items = []
            for hh in range(2):
                for n in range(4):
                    nkt = 4 * n + 4
                    for kt in range(nkt):
                        items.append((hh, n, kt, nkt))
            DEPTH = 4
            sts = [None] * len(items)

            def mfront(j, i=i, items=items, sts=sts):
                hh, n, kt, nkt = items[j]
                mq = max(0, kt - 4 * n)
                c0 = mq * 128
                ks = slice(kt * 128, (kt + 1) * 128)
                qs = slice(n * 512 + c0, (n + 1) * 512)
                Sb, Sres = next_bank()
                P.pe(lambda e: e.matmul(Sb[:, c0:512], lhsT=k96[0:96, hh, ks], rhs=q96[0:96, hh, qs], start=True, stop=True),
                     reads=[("k96", hh, kt // 4), ("q96", hh, n)], writes=[Sres])
                pq = pctr[0] % 6
                pctr[0] += 1
                P.act(lambda e: e.activation(out=pTm[:, pq, c0:512], in_=Sb[:, c0:512], func=AF.Exp, scale=scale_m),
                      reads=[Sres], writes=[("pTm", pq)])
                if kt >= 4 * n:
                    P.pool(lambda e: e.affine_select(out=pTm[:, pq, c0:c0 + 128], in_=pTm[:, pq, c0:c0 + 128],
                                                     pattern=[[1, 128]], compare_op=ALU.is_ge, fill=0.0, base=0,
                                                     channel_multiplier=-1),
                           reads=[("pTm", pq)], writes=[("pTm", pq)])
                sts[j] = (pq, c0)

            def mback(j, i=i, items=items, sts=sts):
                hh, n, kt, nkt = items[j]
                pq, c0 = sts[j]
                hs = slice(hh * 64, hh * 64 + 64)
                cs = slice(n * 512, (n + 1) * 512)
                Ob, Ores = bank(n % 2)
                P.pe(lambda e: e.matmul(Ob[:, c0:512], lhsT=vp[:, kt, 2 * i + hh, :], rhs=pTm[:, pq, c0:512],
                                        start=(kt == 0), stop=(kt == nkt - 1)),
                     reads=[("vp", kt), ("pTm", pq)], writes=[Ores])
                if kt == nkt - 1:
                    q2 = n % 2
                    rres = ("recm", q2)
                    P.act(lambda e: e.activation(out=recm[:, q2, :], in_=Ob[64:128, :], func=AF.Ln), reads=[Ores], writes=[rres])
                    P.act(lambda e: e.activation(out=recm[:, q2, :], in_=recm[:, q2, :], func=AF.Exp, scale=-1.0), reads=[rres], writes=[rres])
                    P.dve(lambda e: e.tensor_tensor(out=actT[hs, i, cs], in0=Ob[0:64, :], in1=recm[:, q2, :], op=ALU.mult),
                          reads=[Ores, rres], writes=[hT_res(i, n)])

            for j in range(min(DEPTH, len(items))):
                mfront(j)
            for j in range(len(items)):
                if j + DEPTH < len(items):
                    mfront(j + DEPTH)
                mback(j)
        AR.release(m0)
        if "aT" in dumps:
            dump("aT", actT, [128, 8, S], BF16, [hT_res(j, n) for j in range(8) for n in range(4)])
        if stop_at <= 4:
            return
        phase_out(actT, actT_tile_res, 8, layer, 7, 0, srcT, srcR, dstT, dstR, next_norm=next_norm_of(sub),
                  pre=[(layer, 9, 8 * 512), (layer, 10, 8 * 512)])

    def xattn(layer, sub):
        srcT, srcR = x_src(sub)
        dstT, dstR = x_dst(sub)
        gb = layer * GC
        m0 = AR.mark()
        memT = AR.alloc([128, 8, 256], BF16)
        phase_norm(lambda tt: mem_in[tt], lambda tt: ("memin", tt), 2, gb + 24, memT, lambda j, g: ("memT", j), width_tiles=2)
        if sub == 0:
            phase_norm(srcT, srcR, NT, gb + 8, actT, hT_res)
        qT = AR.alloc([128, 8, S], BF16)
        kT = AR.alloc([128, 8, 256], BF16)
        vx = AR.alloc([128, 2, DM], BF16)
        pTx = AR.alloc([128, 4, 512], BF16)
        recx = AR.alloc([128, 2, 512], F32)
        bctr = [0]

        def nb():
            b = 2 + (bctr[0] % 6)
            bctr[0] += 1
            return bank(b)

        for hf in range(2):
            w, wr = load_piece(layer, 9 + hf, 8 * 512)
            wv = wview(w, 8, 512)
            for mc in range(4):
                m = hf * 4 + mc
                for n in range(4):
                    cs = slice(n * 512, (n + 1) * 512)
                    pb, pres = nb()
                    for j in range(8):
                        P.pe(lambda e, j=j, pb=pb, mc=mc, cs=cs, wv=wv: e.matmul(pb, lhsT=wv[:, j, mc * 128:(mc + 1) * 128], rhs=actT[:, j, cs],
                                                                               start=(j == 0), stop=(j == 7)), reads=[hT_res(j, n), wr], writes=[pres])
                    evac_scaled(qT[:, m, cs], pb, None, reads=[pres], writes=[("qT", m, n)])
        for hf in range(2):
            w, wr = load_piece(layer, 11 + hf, 8 * 512)
            wv = wview(w, 8, 512)
            for mc in range(4):
                m = hf * 4 + mc
                pb, pres = nb()
                for j in range(8):
                    P.pe(lambda e, j=j, pb=pb, mc=mc, wv=wv: e.matmul(pb[:, 0:256], lhsT=wv[:, j, mc * 128:(mc + 1) * 128], rhs=memT[:, j, :],
                                                                    start=(j == 0), stop=(j == 7)), reads=[("memT", j), wr], writes=[pres])
                evac_scaled(kT[:, m, :], pb[:, 0:256], None, reads=[pres], writes=[("kT", m)])
        for hf in range(2):
            w, wr = load_piece(layer, 13 + hf, 8 * 512)
            wv = wview(w, 8, 512)
            for mt in range(2):
                pb, pres = nb()
                for j in range(8):
                    P.pe(lambda e, j=j, pb=pb, mt=mt, wv=wv: e.matmul(pb, lhsT=memT[:, j, mt * 128:(mt + 1) * 128], rhs=wv[:, j, :],
                                                                    start=(j == 0), stop=(j == 7)), reads=[("memT", j), wr], writes=[pres])
                evac_scaled(vx[:, mt, hf * 512:(hf + 1) * 512], pb, None, reads=[pres], writes=[("vx", mt, hf)])
        prefetch(layer, 15, 8 * 512)
        prefetch(layer, 16, 8 * 512)
        pctr = [0]
        groups = [(h, n) for h in range(4) for n in range(4)]
        xst = [None] * len(groups)

        def xfront(g):
            h, n = groups[g]
            cs = slice(n * 512, (n + 1) * 512)
            pqs = []
            for mt in range(2):
                Sb, Sres = nb()
                for c in range(2):
                    P.pe(lambda e, Sb=Sb, c=c, mt=mt: e.matmul(Sb, lhsT=kT[:, 2 * h + c, mt * 128:(mt + 1) * 128], rhs=qT[:, 2 * h + c, cs],
                                                             start=(c == 0), stop=(c == 1)),
                         reads=[("kT", 2 * h + c), ("qT", 2 * h + c, n)], writes=[Sres])
                pq = pctr[0] % 4
                pctr[0] += 1
                pqs.append(pq)
                P.act(lambda e, Sb=Sb, pq=pq: e.activation(out=pTx[:, pq, :], in_=Sb, func=AF.Exp, scale=1.0 / 16.0),
                      reads=[Sres], writes=[("pTx", pq)])
            xst[g] = pqs

        def xback(g):
            h, n = groups[g]
            cs = slice(n * 512, (n + 1) * 512)
            pqs = xst[g]
            Db, Dres = nb()
            for mt in range(2):
                P.pe(lambda e, mt=mt, pq=pqs[mt]: e.matmul(Db, lhsT=ones_bf, rhs=pTx[:, pq, :], start=(mt == 0), stop=(mt == 1)),
                     reads=["ones", ("pTx", pqs[mt])], writes=[Dres])
            q2 = g % 2
            rres = ("recx", q2)
            P.act(lambda e: e.activation(out=recx[:, q2, :], in_=Db, func=AF.Ln), reads=[Dres], writes=[rres])
            P.act(lambda e: e.activation(out=recx[:, q2, :], in_=recx[:, q2, :], func=AF.Exp, scale=-1.0), reads=[rres], writes=[rres])
            for c in range(2):
                Ob, Ores = nb()
                m = 2 * h + c
                for mt in range(2):
                    P.pe(lambda e, Ob=Ob, mt=mt, m=m, pq=pqs[mt]: e.matmul(Ob, lhsT=vx[:, mt, m * 128:(m + 1) * 128], rhs=pTx[:, pq, :],
                                                                         start=(mt == 0), stop=(mt == 1)),
                         reads=[("vx", mt, m // 4), ("pTx", pqs[mt])], writes=[Ores])
                P.dve(lambda e, Ob=Ob, m=m: e.tensor_tensor(out=actT[:, m, cs], in0=Ob, in1=recx[:, q2, :], op=ALU.mult),
                      reads=[Ores, rres], writes=[hT_res(m, n)])

        xfront(0)
        for g in range(len(groups)):
            if g + 1 < len(groups):
                xfront(g + 1)
            xback(g)
        AR.release(m0)
        phase_out(actT, actT_tile_res, 8, layer, 15, 1, srcT, srcR, dstT, dstR, next_norm=next_norm_of(sub),
                  pre=[(layer, 17, 8 * 512), (layer, 18, 8 * 512)])

    def mlp(layer, sub):
        srcT, srcR = x_src(sub)
        dstT, dstR = x_dst(sub)
        gb = layer * GC
        if sub == 0:
            phase_norm(srcT, srcR, NT, gb + 16, actT, hT_res)
        m0 = AR.mark()
        hid = AR.alloc([128, 2, 4, S], BF16)
        yacc = AR.alloc([128, NT, DM], F32)
        rl = AR.alloc([128, 2, 512], F32)
        bctr = [0]

        def nb():
            b = 2 + (bctr[0] % 6)
            bctr[0] += 1
            return bank(b)

        def up(g):
            w, wr = load_piece(layer, 17 + g, 8 * 512)
            wv = wview(w, 8, 512)
            hb = g % 2
            for fc in range(4):
                for n in range(4):
                    cs = slice(n * 512, (n + 1) * 512)
                    pb, pres = nb()
                    for j in range(8):
                        P.pe(lambda e, j=j, pb=pb, fc=fc, cs=cs, wv=wv: e.matmul(pb, lhsT=wv[:, j, fc * 128:(fc + 1) * 128], rhs=actT[:, j, cs],
                                                                               start=(j == 0), stop=(j == 7)), reads=[hT_res(j, n), wr], writes=[pres])
                    q2 = bctr[0] % 2
                    P.act(lambda e, pb=pb, q2=q2: e.activation(out=rl[:, q2, :], in_=pb, func=AF.Relu), reads=[pres], writes=[("rl", q2)])
                    P.dve(lambda e, pb=pb, q2=q2, hb=hb, fc=fc, cs=cs: e.tensor_tensor(out=hid[:, hb, fc, cs], in0=pb, in1=rl[:, q2, :], op=ALU.mult),
                          reads=[pres, ("rl", q2)], writes=[("hid", hb, fc, n)])

        def down(g):
            hb = g % 2
            ws = []
            for hf in range(2):
                w, wr = load_piece(layer, 25 + 2 * g + hf, 4 * 512)
                ws.append((wview(w, 4, 512), wr))
            if g == 7:
                prefetch(layer + 1, 0, 8 * 512)
                prefetch(layer + 1, 1, 8 * 512)
            for tt in range(NT):
                for hf in range(2):
                    pb, pres = nb()
                    wv, wr = ws[hf]
                    for k in range(4):
                        P.pe(lambda e, k=k, pb=pb, tt=tt, wv=wv, hb=hb: e.matmul(pb, lhsT=hid[:, hb, k, tt * 128:(tt + 1) * 128], rhs=wv[:, k, :],
                                                                               start=(k == 0), stop=(k == 3)),
                             reads=[("hid", hb, k, tt // 4), wr], writes=[pres])
                    ya = yacc[:, tt, hf * 512:(hf + 1) * 512]
                    yr = ("yacc", tt, hf)
                    if g == 0:
                        P.act(lambda e, ya=ya, pb=pb: e.copy(out=ya, in_=pb), reads=[pres], writes=[yr])
                    else:
                        P.dve(lambda e, ya=ya, pb=pb: e.tensor_tensor(out=ya, in0=ya, in1=pb, op=ALU.add), reads=[pres, yr], writes=[yr])
                if g == 7:
                    out_tile(tt)

        up(0)
        out_tile = out_finish = None
        for g in range(8):
            if g + 1 < 8:
                up(g + 1)
            if g == 7:
                out_tile, out_finish = phase_out(None, None, 0, layer, None, 2, srcT, srcR, dstT, dstR,
                                                 acc=(yacc, lambda tt: [("yacc", tt, 0), ("yacc", tt, 1)]), next_norm=next_norm_of(sub),
                                                 pre=(), deferred=True)
            down(g)
        out_finish()
        AR.release(m0)

    subs = []
    for l in range(NL):
        subs += [(mixer, l), (xattn, l), (mlp, l)]
    for sub in range(n_sub):
        fn, l = subs[sub]
        fn(l, sub)
    if stop_at < 99:
        P.dma("sp", "dbgcopy", lambda e: e.dma_start(out=out[0], in_=x_in[0]), writes=["dbgout"])
    P.emit(st)
    st.close()
    return nc, P, AR, dump_out


_PROGRAM_CACHE = {}


def kernel(**inputs):
    inputs = {k: np.asarray(v) for k, v in inputs.items()}
    B = inputs["x"].shape[0]
    shared = prep_shared(inputs)
    if "full" not in _PROGRAM_CACHE:
        _PROGRAM_CACHE["full"] = build_program(6)[0]
    nc = _PROGRAM_CACHE["full"]
    in_maps = []
    for b in range(B):
        m = dict(shared)
        m["x"] = np.ascontiguousarray(inputs["x"][b].reshape(NT, 128, DM))
        m["mem"] = np.ascontiguousarray(inputs["mem"][b].reshape(2, 128, DM))
        m["pos"] = np.ascontiguousarray(np.broadcast_to(inputs["positions"][b].astype(np.int32)[None, :], (128, S)))
        in_maps.append(m)
    res = run_bass_kernel_spmd(nc, in_maps, core_ids=list(range(B)))
    outs = [np.asarray(r["out"]).reshape(S, DM) for r in res.results]
    return np.stack(outs, axis=0).astype(np.float32)
```

```python
import numpy as np
import concourse.bass as bass
import concourse.mybir as mybir
from concourse.bass_utils import run_bass_kernel_spmd

F32 = mybir.dt.float32
BF16 = mybir.dt.bfloat16
I32 = mybir.dt.int32
ALU = mybir.AluOpType
AF = mybir.ActivationFunctionType
AX = mybir.AxisListType


class Prog:
    ENGS = ("pe", "act", "dve", "pool", "sp")

    def __init__(self, nc, same_engine_sync=("act", "dve", "pool")):
        self.nc = nc
        self.ops = []
        self.same_engine_sync = set(same_engine_sync)
        self.nbar = 0
        self.bar_at = []

    def op(self, eng, fn, reads=(), writes=(), chan=None, free=False):
        def _flat(xs):
            o = []
            for x in xs:
                if isinstance(x, list):
                    o.extend(x)
                else:
                    o.append(x)
            return o
        reads = _flat(reads)
        writes = _flat(writes)
        for a in list(reads):
            if isinstance(a, tuple) and a and a[0] == "pb" and a not in writes:
                writes.append(a)
        self.ops.append(dict(eng=eng, fn=fn, reads=tuple(reads), writes=tuple(writes), chan=chan, free=free,
                             bar=self.nbar))

    def barrier(self):
        self.nbar += 1
        self.bar_at.append(len(self.ops))

    def pe(self, fn, reads=(), writes=()):
        self.op("pe", fn, reads, writes)

    def act(self, fn, reads=(), writes=()):
        self.op("act", fn, reads, writes)

    def dve(self, fn, reads=(), writes=()):
        self.op("dve", fn, reads, writes)

    def pool(self, fn, reads=(), writes=()):
        self.op("pool", fn, reads, writes)

    def dma(self, eng, chan, fn, reads=(), writes=(), free=False):
        self.op(eng, fn, reads, writes, chan=chan, free=free)

    def emit(self, stack):
        nc = self.nc
        ops = self.ops
        n = len(ops)
        last_writer = {}
        readers = {}
        deps = [None] * n
        last_eng = {}
        last_chan = {}
        bar_deps = set()
        bar_ptr = 0
        for i, o in enumerate(ops):
            while bar_ptr < len(self.bar_at) and self.bar_at[bar_ptr] <= i:
                bar_deps = set(last_eng.values()) | set(last_chan.values())
                bar_ptr += 1
            if o["chan"] is None:
                last_eng[o["eng"]] = i
            else:
                last_chan[o["chan"]] = i
            d = set()
            if not o["free"]:
                d |= bar_deps
            for a in o["reads"]:
                w = last_writer.get(a)
                if w is not None:
                    d.add(w)
            for a in o["writes"]:
                w = last_writer.get(a)
                if w is not None:
                    d.add(w)
                for r in readers.get(a, ()):
                    d.add(r)
            d.discard(i)
            deps[i] = d
            for a in o["reads"]:
                readers.setdefault(a, []).append(i)
            for a in o["writes"]:
                last_writer[a] = i
                readers[a] = []
        has_dependents = [False] * n
        for i in range(n):
            o = ops[i]
            for j in deps[i]:
                p = ops[j]
                if p["chan"] is None and p["eng"] == o["eng"] and o["chan"] is None \
                        and p["eng"] not in self.same_engine_sync:
                    continue
                has_dependents[j] = True
        final_waits = []
        eng_count = {e: 0 for e in self.ENGS}
        chan_count = {}
        sig = [None] * n
        for i, o in enumerate(ops):
            if o["chan"] is not None:
                c = ("chan", o["chan"])
                chan_count[c] = chan_count.get(c, 0) + 16
                sig[i] = (c, chan_count[c])
            elif has_dependents[i]:
                c = ("eng", o["eng"])
                eng_count[o["eng"]] += 1
                sig[i] = (c, eng_count[o["eng"]])
        semkeys = [("eng", e) for e in self.ENGS if eng_count[e] > 0] + sorted(chan_count.keys(), key=str)
        sems = {}
        for k in semkeys:
            sems[k] = stack.enter_context(nc.semaphore("s" + str(len(sems))))
        self.n_sems = len(sems)
        per_eng = {e: [] for e in self.ENGS}
        for i, o in enumerate(ops):
            per_eng[o["eng"]].append(i)
        waits = [None] * n
        waited = {e: {} for e in self.ENGS}
        for i, o in enumerate(ops):
            e = o["eng"]
            need = {}
            for j in deps[i]:
                p = ops[j]
                if sig[j] is None:
                    continue
                if p["chan"] is None and o["chan"] is None and p["eng"] == e \
                        and e not in self.same_engine_sync:
                    continue
                k, v = sig[j]
                if need.get(k, 0) < v:
                    need[k] = v
            w = []
            for k, v in need.items():
                if waited[e].get(k, 0) >= v:
                    continue
                waited[e][k] = v
                w.append((k, v))
            waits[i] = w
        final = [(k, v) for k, v in chan_count.items()]
        self.n_waits = sum(len(w) for w in waits)

        block = stack.enter_context(nc.Block())
        engobj = {"pe": "tensor", "act": "scalar", "dve": "vector", "pool": "gpsimd", "sp": "sync"}

        def make(e):
            idxs = per_eng[e]

            def body(eng):
                for i in idxs:
                    o = ops[i]
                    for k, v in waits[i]:
                        eng.wait_ge(sems[k], v)
                    ins = o["fn"](eng)
                    if sig[i] is not None:
                        k, v = sig[i]
                        if o["chan"] is not None:
                            ins.then_inc(sems[k], 16)
                        else:
                            ins.then_inc(sems[k], 1)
                if e == "sp":
                    for k, v in final:
                        eng.wait_ge(sems[k], v)
                    for ee in self.ENGS:
                        if eng_count[ee] > 0:
                            eng.wait_ge(sems[("eng", ee)], eng_count[ee])
            return body

        for e in self.ENGS:
            if per_eng[e] or e == "sp":
                getattr(block, engobj[e])(make(e))


S = 2048
DM = 1024
NT = 16
NL = 2
NPIECE = 41
GC = 40
EPS = 1e-6
PI = float(np.pi)
TWO_PI = float(2.0 * np.pi)
DILS = (1, 4, 16)
NEG = -30000.0
RING = 4
SLOT = 4096


def _t5_bucket(d):
    d = np.maximum(d, 0)
    df = np.maximum(d.astype(np.float32), np.float32(1.0))
    large = 16 + (np.log(df / np.float32(16)) / np.float32(np.log(2048 / 16)) * np.float32(16)).astype(np.int32)
    large = np.minimum(large, 31)
    return np.where(d < 16, d, large)


def _std(w):
    K, N = w.shape
    return np.ascontiguousarray(w.reshape(K // 128, 128, N).transpose(1, 0, 2))


def _piece(a):
    flat = a.reshape(128, -1)
    out = np.zeros((128, SLOT), np.float32)
    out[:, : flat.shape[1]] = flat
    return out


def prep_shared(inp):
    pieces = []
    gcol = np.zeros((128, NL * GC), np.float32)
    grow = np.zeros((NL, 6, 128, DM), np.float32)
    for l in range(NL):
        w_in = np.asarray(inp["w_in"][l])
        cols = [w_in[:, 0:640]]
        kr = w_in[:, 640:672]
        krs = np.concatenate([kr[:, 16:32], kr[:, 0:16]], axis=1)
        cols.append(np.concatenate([kr] * 4, axis=1))
        cols.append(np.concatenate([krs] * 4, axis=1))
        cols.append(w_in[:, 672:1696])
        wA = np.concatenate(cols, axis=1)
        assert wA.shape[1] == 1920
        wA = _std(wA)
        for i in range(4):
            pieces.append(_piece(wA[:, :, i * 512:(i + 1) * 512]))
        pieces.append(_piece(_std(w_in[:, 1696:2208])))
        w_uq = np.asarray(inp["w_uq"][l]).reshape(384, 8, 96)
        nope = w_uq[:, :, 0:64].reshape(384, 512)
        rope = w_uq[:, :, 64:96]
        ropes = np.concatenate([rope[:, :, 16:32], rope[:, :, 0:16]], axis=2)
        uqA = np.concatenate([nope, rope.reshape(384, 256), ropes.reshape(384, 256)], axis=1)
        pieces.append(_piece(_std(uqA)))
        w_ukv = np.asarray(inp["w_ukv"][l]).reshape(256, 8, 128)
        ukvA = np.concatenate([w_ukv[:, :, 0:64].reshape(256, 512), w_ukv[:, :, 64:128].reshape(256, 512)], axis=1)
        pieces.append(_piece(_std(ukvA)))
        for name in ("w_mix_out", "w_xq", "w_xk", "w_xv", "w_xo"):
            w = _std(np.asarray(inp[name][l]))
            pieces.append(_piece(w[:, :, 0:512]))
            pieces.append(_piece(w[:, :, 512:1024]))
        w = _std(np.asarray(inp["w_up"][l]))
        for i in range(8):
            pieces.append(_piece(w[:, :, i * 512:(i + 1) * 512]))
        w = _std(np.asarray(inp["w_down"][l]))
        for g in range(8):
            for hf in range(2):
                pieces.append(_piece(w[:, g * 4:(g + 1) * 4, hf * 512:(hf + 1) * 512]))
        b = l * GC
        for off, name, nk in ((0, "pre_mix_g", 8), (8, "pre_xattn_g", 8), (16, "pre_mlp_g", 8), (24, "mem_g", 8),
                              (32, "q_norm_g", 3), (35, "kv_norm_g", 2)):
            gcol[:, b + off:b + off + nk] = np.asarray(inp[name][l]).reshape(nk, 128).T
        for i, name in enumerate(("post_mix_g", "post_xattn_g", "post_mlp_g", "pre_mix_g", "pre_xattn_g", "pre_mlp_g")):
            grow[l, i] = np.broadcast_to(np.asarray(inp[name][l])[None, :], (128, DM))
    wts = np.stack(pieces, axis=0)
    rb = np.asarray(inp["rel_bias"])
    ext = np.concatenate([rb, np.full((1, 8), NEG, np.float32)], axis=0)
    ki = np.arange(128)[:, None]
    qi = np.arange(128)[None, :]
    btab = np.zeros((128, 3, 4, 2, 2, 128), np.float32)
    for p, dil in enumerate(DILS):
        for kb in range(2):
            rel = (128 if kb == 0 else 0) + qi - ki
            valid = (rel >= 0) & (rel <= 128)
            idx = np.where(valid, _t5_bucket(rel * dil), 32)
            for h in range(8):
                btab[:, p, h // 2, kb, h % 2, :] = ext[idx, h]
    rconst = np.zeros((128, 4), np.float32)
    prow = np.arange(128)
    rconst[:, 0] = (np.float32(10000.0) ** (-(prow % 16).astype(np.float32) / np.float32(16))).astype(np.float32)
    rconst[:, 1] = PI / 2
    rconst[:, 2] = np.where((prow % 32) < 16, PI, 0.0)
    return dict(wts=wts, gcol=gcol, grow=grow, btab=btab.reshape(128, -1), rconst=rconst)


class Arena:
    def __init__(self, ap, nbytes):
        self.ap = ap
        self.nbytes = nbytes
        self.top = 0
        self.peak = 0

    def alloc(self, shape, dt):
        es = 2 if dt == BF16 else 4
        n = 1
        for s in shape[1:]:
            n *= s
        nb = (n * es + 63) // 64 * 64
        off = self.top
        self.top += nb
        self.peak = max(self.peak, self.top)
        assert self.top <= self.nbytes, f"arena overflow {self.top} > {self.nbytes}"
        ap = self.ap[:, off // 2:(off + n * es) // 2]
        if dt != BF16:
            ap = ap.bitcast(dt)
        if len(shape) == 3:
            ap = ap.rearrange("p (a b) -> p a b", b=shape[2])
        elif len(shape) == 4:
            ap = ap.rearrange("p (a b c) -> p a b c", b=shape[2], c=shape[3])
        elif len(shape) == 5:
            ap = ap.rearrange("p (a b c d) -> p a b c d", b=shape[2], c=shape[3], d=shape[4])
        if shape[0] != 128:
            ap = ap[0:shape[0]]
        return ap

    def mark(self):
        return self.top

    def release(self, m):
        self.top = m


def build_program(n_sub=6, dumps=(), stop_at=99):
    from contextlib import ExitStack
    nc = bass.Bass("TRN2", target_bir_lowering=False)
    x_in = nc.dram_tensor("x", [NT, 128, DM], F32, kind="ExternalInput").ap()
    mem_in = nc.dram_tensor("mem", [2, 128, DM], F32, kind="ExternalInput").ap()
    pos_in = nc.dram_tensor("pos", [128, S], I32, kind="ExternalInput").ap()
    wts = nc.dram_tensor("wts", [NL * NPIECE, 128, SLOT], F32, kind="ExternalInput").ap()
    gcol_in = nc.dram_tensor("gcol", [128, NL * GC], F32, kind="ExternalInput").ap()
    grow_in = nc.dram_tensor("grow", [NL, 6, 128, DM], F32, kind="ExternalInput").ap()
    btab_in = nc.dram_tensor("btab", [128, 3 * 4 * 2 * 2 * 128], F32, kind="ExternalInput").ap()
    rconst_in = nc.dram_tensor("rconst", [128, 4], F32, kind="ExternalInput").ap()
    out = nc.dram_tensor("out", [NT, 128, DM], F32, kind="ExternalOutput").ap()
    xs = nc.dram_tensor("xs_scratch", [NT, 128, DM], F32).ap()
    dump_out = {}

    st = ExitStack()
    ARENA_BYTES = 206 * 1024
    arena_t = st.enter_context(nc.sbuf_tensor("arena", [128, ARENA_BYTES // 2], BF16))
    AR = Arena(arena_t[:, :], ARENA_BYTES)
    PS = [st.enter_context(nc.psum_tensor(f"ps{i}", [128, 1024], F32)) for i in range(4)]
    P = Prog(nc, same_engine_sync=("act", "dve", "pool"))

    def bank(i):
        return PS[i // 2][:, (i % 2) * 512:(i % 2) * 512 + 512], ("pb", i)

    ident = AR.alloc([128, 128], F32)
    ones_bf = AR.alloc([128, 128], BF16)
    ident_bf = AR.alloc([128, 128], BF16)
    gcol = AR.alloc([128, NL * GC], F32)
    rconst = AR.alloc([128, 4], F32)
    ring = AR.alloc([128, RING, SLOT], BF16)
    junk = AR.alloc([128, DM], BF16)
    stat = AR.alloc([128, 64], F32)
    epst = AR.alloc([128, 1], F32)
    actT = AR.alloc([128, 8, S], BF16)
    TC = AR.alloc([64, S], BF16)
    TS = AR.alloc([64, S], BF16)
    phase_mark = AR.mark()

    uid = [0]

    def U(name):
        uid[0] += 1
        return (name, uid[0])

    P.pool(lambda e: e.memset(ident, 0.0), writes=["ident"])
    P.pool(lambda e: e.affine_select(out=ident, in_=ident, pattern=[[-1, 128]], compare_op=ALU.not_equal,
                                     fill=1.0, base=0, channel_multiplier=1), reads=["ident"], writes=["ident"])
    P.pool(lambda e: e.memset(ones_bf, 1.0), writes=["ones"])
    P.pool(lambda e: e.memset(epst, EPS), writes=["epst"])
    P.dve(lambda e: e.tensor_copy(out=ident_bf, in_=ident), reads=["ident"], writes=["identb"])
    P.dma("sp", "gcol", lambda e: e.dma_start(out=gcol, in_=gcol_in), writes=["gcol"])
    P.dma("sp", "rconst", lambda e: e.dma_start(out=rconst, in_=rconst_in), writes=["rconst"])

    piece_ctr = [0]

    prefetched = {}

    def prefetch(layer, idx, nelem):
        if layer < NL and (layer, idx) not in prefetched:
            prefetched[(layer, idx)] = load_piece(layer, idx, nelem)

    def load_piece(layer, idx, nelem):
        if (layer, idx) in prefetched:
            return prefetched.pop((layer, idx))
        k = piece_ctr[0]
        piece_ctr[0] += 1
        slot = k % RING
        res = ("ring", slot)
        src = wts[layer * NPIECE + idx, :, 0:nelem].rearrange("p (a b) -> p a b", b=1024)
        dst = ring[:, slot, 0:nelem].rearrange("p (a b) -> p a b", b=1024)
        P.dma("pool", ("ring", slot), lambda e: e.dma_start(out=dst, in_=src), writes=[res])
        return ring[:, slot, :], res

    def wview(slot_ap, nk, ncol):
        return slot_ap[:, 0:nk * ncol].rearrange("p (a b) -> p a b", b=ncol)

    grow_ctr = [0]

    def load_grow(gbuf, k, layer, which):
        res = ("grow", k)
        P.dma("sp", ("grow", k), lambda e: e.dma_start(out=gbuf[:, k, :], in_=grow_in[layer, which]), writes=[res])
        return gbuf[:, k, :], res

    def dump(name, ap, shape, dt, reads):
        t = nc.dram_tensor("dbg_" + name, list(shape), dt, kind="ExternalOutput").ap()
        dump_out[name] = t
        P.dma("sp", ("dump", name), lambda e: e.dma_start(out=t, in_=ap), reads=reads)

    def rstd_from_ss(ss_ap, res, n, scale):
        P.act(lambda e: e.activation(out=ss_ap, in_=ss_ap, func=AF.Ln, scale=scale, bias=epst[:, 0:1]), reads=[res, "epst"], writes=[res])
        P.act(lambda e: e.activation(out=ss_ap, in_=ss_ap, func=AF.Exp, scale=-0.5), reads=[res], writes=[res])

    evac_flip = [0]

    def evac_scaled(out_ap, in_ap, scale_ap, reads, writes):
        evac_flip[0] ^= 1
        if evac_flip[0]:
            if scale_ap is None:
                P.act(lambda e: e.copy(out=out_ap, in_=in_ap), reads=reads, writes=writes)
            else:
                P.act(lambda e: e.activation(out=out_ap, in_=in_ap, func=AF.Copy, scale=scale_ap), reads=reads, writes=writes)
        else:
            if scale_ap is None:
                P.dve(lambda e: e.tensor_copy(out=out_ap, in_=in_ap), reads=reads, writes=writes)
            else:
                P.dve(lambda e: e.tensor_scalar(out=out_ap, in0=in_ap, scalar1=scale_ap, scalar2=None, op0=ALU.mult),
                      reads=reads, writes=writes)

    def phase_norm(src_tile, src_res, ntiles, gbase, dstT, dst_res_fn, width_tiles=4):
        m = AR.mark()
        xt = AR.alloc([128, 4, DM], F32)
        xb = AR.alloc([128, 4, DM], BF16)
        PSbf = [PS[0][:, 0:512].bitcast(BF16), PS[0][:, 512:1024].bitcast(BF16)]
        ssr = ("stat", "norm")
        ngroups = (ntiles + width_tiles - 1) // width_tiles
        for g in range(ngroups):
            tts = list(range(g * width_tiles, min(ntiles, (g + 1) * width_tiles)))
            for q, tt in enumerate(tts):
                P.dma("sp", ("xt", q), lambda e, q=q, tt=tt: e.dma_start(out=xt[:, q, :], in_=src_tile(tt)),
                      reads=[src_res(tt)], writes=[("xt", q)])
                P.act(lambda e, q=q: e.activation(out=junk, in_=xt[:, q, :], func=AF.Square, accum_out=stat[:, q:q + 1]),
                      reads=[("xt", q)], writes=["junk", (ssr, q)])
            nq = len(tts)
            sres = [(ssr, q) for q in range(nq)]
            P.dve(lambda e, nq=nq: e.tensor_scalar(out=stat[:, 0:nq], in0=stat[:, 0:nq], scalar1=1.0 / DM, scalar2=EPS,
                                                   op0=ALU.mult, op1=ALU.add), reads=sres, writes=sres)
            P.act(lambda e, nq=nq: e.activation(out=stat[:, 0:nq], in_=stat[:, 0:nq], func=AF.Ln), reads=sres, writes=sres)
            P.act(lambda e, nq=nq: e.activation(out=stat[:, 0:nq], in_=stat[:, 0:nq], func=AF.Exp, scale=-0.5), reads=sres, writes=sres)
            for q in range(nq):
                P.dve(lambda e, q=q: e.tensor_scalar(out=xb[:, q, :], in0=xt[:, q, :], scalar1=stat[:, q:q + 1], scalar2=None,
                                                     op0=ALU.mult), reads=[("xt", q), (ssr, q)], writes=[("xb", q)])
            for j in range(8):
                pbf = PSbf[j % 2]
                pres = ("pb", j % 2)
                for q in range(nq):
                    P.pe(lambda e, q=q, j=j, pbf=pbf: e.transpose(out=pbf[:, q * 128:(q + 1) * 128], in_=xb[:, q, j * 128:(j + 1) * 128],
                                                                 identity=ident_bf), reads=[("xb", q), "identb"], writes=[pres])
                t0 = tts[0] * 128
                evac_scaled(dstT[:, j, t0:t0 + nq * 128], pbf[:, 0:nq * 128], gcol[:, gbase + j:gbase + j + 1],
                            reads=[pres, "gcol"], writes=[dst_res_fn(j, g)])
        AR.release(m)
        P.barrier()

    def phase_out(featT, feat_res_fn, nk, layer, piece_idx, grow_which, src_tile, src_res, dst_tile, dst_res, acc=None,
                  next_norm=None, pre=(), deferred=False, nobarrier=False):
        m = AR.mark()
        if not deferred and not nobarrier:
            P.barrier()
        ADDSPLIT = 640
        ynres = lambda q: [("yn", q, 0), ("yn", q, 1)]
        xt4 = AR.alloc([128, 4, DM], F32)
        xt = xt4[:, 0:2, :]
        yn = xt4[:, 2:4, :]
        gbuf = AR.alloc([128, 2, DM], F32)
        g_ap, g_res = load_grow(gbuf, 0, layer, grow_which)
        if next_norm is not None:
            xbn = AR.alloc([128, 3, DM], BF16)
            g2_ap, g2_res = load_grow(gbuf, 1, next_norm[0], next_norm[1])
            PSbf = [PS[0][:, 0:512].bitcast(BF16), PS[0][:, 512:1024].bitcast(BF16)]
        if acc is None:
            w0, w0r = load_piece(layer, piece_idx, 8 * 512)
            w1, w1r = load_piece(layer, piece_idx + 1, 8 * 512)
            wv = [wview(w0, 8, 512), wview(w1, 8, 512)]
            wr = [w0r, w1r]
        for (pl, pi, pn) in pre:
            prefetch(pl, pi, pn)

        def tail_front(tt):
            q = tt % 2
            q3 = tt % 3
            s2 = ("stat", "out2", q)
            P.act(lambda e: e.activation(out=junk, in_=yn[:, q, :], func=AF.Square, accum_out=stat[:, 10 + q:11 + q]),
                  reads=ynres(q), writes=["junk", s2])
            rstd_from_ss(stat[:, 10 + q:11 + q], s2, 1, 1.0 / DM)
            P.dve(lambda e: e.scalar_tensor_tensor(out=xbn[:, q3, :], in0=yn[:, q, :], scalar=stat[:, 10 + q:11 + q], in1=g2_ap,
                                                   op0=ALU.mult, op1=ALU.mult),
                  reads=ynres(q) + [s2, g2_res], writes=[("xbn", q3)])

        def tail_pe(tt):
            q = tt % 3
            for half in range(2):
                for jj in range(4):
                    j = half * 4 + jj
                    P.pe(lambda e, j=j, jj=jj, half=half: e.transpose(out=PSbf[half][:, jj * 128:(jj + 1) * 128],
                                                                        in_=xbn[:, q, j * 128:(j + 1) * 128], identity=ident_bf),
                         reads=[("xbn", q), "identb"], writes=[("pb", half)])
                outv = actT[:, half * 4:half * 4 + 4, tt * 128:(tt + 1) * 128]
                inv = PSbf[half][:, 0:512].rearrange("p (a b) -> p a b", a=4)
                wres_ = [("actT", half * 4 + jj, tt) for jj in range(4)]
                if half == 0:
                    P.act(lambda e, outv=outv, inv=inv: e.copy(out=outv, in_=inv), reads=[("pb", half)], writes=wres_)
                else:
                    P.dve(lambda e, outv=outv, inv=inv: e.tensor_copy(out=outv, in_=inv), reads=[("pb", half)], writes=wres_)

        def tile(tt):
            q = tt % 2
            P.dma("sp", ("xt", q), lambda e, q=q, tt=tt: e.dma_start(out=xt[:, q, :], in_=src_tile(tt)),
                  reads=[src_res(tt)], writes=[("xt", q)])
            if acc is None:
                Y = PS[2 + q]
                yres = [("pb", 4 + 2 * q), ("pb", 5 + 2 * q)]
                for hf in range(2):
                    for k in range(nk):
                        P.pe(lambda e, hf=hf, k=k, tt=tt, Y=Y: e.matmul(Y[:, hf * 512:(hf + 1) * 512], lhsT=featT[:, k, tt * 128:(tt + 1) * 128],
                                                                        rhs=wv[hf][:, k, :], start=(k == 0), stop=(k == nk - 1)),
                             reads=[feat_res_fn(k, tt), wr[hf]], writes=[yres[hf]])
                y_ap = Y[:, :]
            else:
                y_ap = acc[0][:, tt, :]
                yres = acc[1](tt)
            if next_norm is not None and tt >= 3:
                tail_pe(tt - 3)
            sres = ("stat", "out", q)
            P.act(lambda e, q=q, y_ap=y_ap: e.activation(out=junk, in_=y_ap, func=AF.Square, accum_out=stat[:, 8 + q:9 + q]),
                  reads=yres, writes=["junk", sres])
            rstd_from_ss(stat[:, 8 + q:9 + q], sres, 1, 1.0 / DM)
            P.dve(lambda e, q=q, y_ap=y_ap: e.scalar_tensor_tensor(out=yn[:, q, :], in0=y_ap, scalar=stat[:, 8 + q:9 + q], in1=g_ap,
                                                                   op0=ALU.mult, op1=ALU.mult),
                  reads=yres + [sres, g_res], writes=ynres(q))
            P.dve(lambda e, q=q: e.tensor_tensor(out=yn[:, q, 0:ADDSPLIT], in0=yn[:, q, 0:ADDSPLIT], in1=xt[:, q, 0:ADDSPLIT], op=ALU.add),
                  reads=[("yn", q, 0), ("xt", q)], writes=[("yn", q, 0)])
            P.pool(lambda e, q=q: e.tensor_tensor(out=yn[:, q, ADDSPLIT:DM], in0=yn[:, q, ADDSPLIT:DM], in1=xt[:, q, ADDSPLIT:DM], op=ALU.add),
                   reads=[("yn", q, 1), ("xt", q)], writes=[("yn", q, 1)])
            P.dma("pool", ("yst", q), lambda e, q=q, tt=tt: e.dma_start(out=dst_tile(tt), in_=yn[:, q, :]),
                  reads=ynres(q), writes=[dst_res(tt)])
            if next_norm is not None and tt >= 1:
                tail_front(tt - 1)
        def finish():
            if next_norm is not None:
                tail_front(NT - 1)
                tail_pe(NT - 3)
                tail_pe(NT - 2)
                tail_pe(NT - 1)
            AR.release(m)
            P.barrier()

        if deferred:
            return tile, finish
        for tt in range(NT):
            tile(tt)
        finish()

    def x_src(sub):
        if sub == 0:
            return (lambda tt: x_in[tt]), (lambda tt: ("xin", tt))
        return (lambda tt: xs[tt]), (lambda tt: ("xs", tt))

    def x_dst(sub):
        if sub == n_sub - 1:
            return (lambda tt: out[tt]), (lambda tt: ("xout", tt))
        return (lambda tt: xs[tt]), (lambda tt: ("xs", tt))

    def next_norm_of(sub):
        if sub + 1 >= n_sub:
            return None
        return ((sub + 1) // 3, 3 + (sub + 1) % 3)

    hT_res = lambda j, g: [("actT", j, 4 * g + t) for t in range(4)]

    def actT_tile_res(k, tt):
        return ("actT", k, tt)

    def mixer(layer, sub):
        srcT, srcR = x_src(sub)
        dstT, dstR = x_dst(sub)
        gb = layer * GC
        if sub == 0:
            phase_norm(srcT, srcR, NT, gb + 0, actT, hT_res)
        if stop_at <= 1:
            return
        m0 = AR.mark()
        craw = AR.alloc([128, 5, S], BF16)
        kpe = AR.alloc([64, S], BF16)
        m1d = AR.mark()
        qdT = AR.alloc([128, 4, S], BF16)
        kdT = AR.alloc([128, 4, S], BF16)
        vdT = AR.alloc([128, 4, S], BF16)
        m1 = AR.mark()
        if layer == 0:
            posi = AR.alloc([64, 512], I32)
            ang = AR.alloc([64, 512], F32)
            kint = AR.alloc([64, 512], I32)
            kf = AR.alloc([64, 512], F32)
        def rope_chunk(n):
            cs = slice(n * 512, (n + 1) * 512)
            P.dma("sp", "posi", lambda e, cs=cs: e.dma_start(out=posi, in_=pos_in[0:64, cs]), writes=["posi"])
            for T, col, tres in ((TC, 1, "TC"), (TS, 2, "TS")):
                P.dve(lambda e: e.tensor_copy(out=ang, in_=posi), reads=["posi"], writes=["ang"])
                P.dve(lambda e, col=col: e.tensor_scalar(out=ang, in0=ang, scalar1=rconst[0:64, 0:1], scalar2=rconst[0:64, col:col + 1],
                                                         op0=ALU.mult, op1=ALU.add), reads=["ang", "rconst"], writes=["ang"])
                P.dve(lambda e: e.tensor_scalar(out=kint, in0=ang, scalar1=1.0 / TWO_PI, scalar2=None, op0=ALU.mult),
                      reads=["ang"], writes=["kint"])
                P.dve(lambda e: e.tensor_copy(out=kf, in_=kint), reads=["kint"], writes=["kf"])
                P.dve(lambda e: e.scalar_tensor_tensor(out=ang, in0=kf, scalar=-TWO_PI, in1=ang, op0=ALU.mult, op1=ALU.add),
                      reads=["kf", "ang"], writes=["ang"])
                P.dve(lambda e: e.tensor_scalar(out=kf, in0=ang, scalar1=PI, scalar2=-TWO_PI, op0=ALU.is_gt, op1=ALU.mult),
                      reads=["ang"], writes=["kf"])
                P.dve(lambda e: e.tensor_tensor(out=ang, in0=ang, in1=kf, op=ALU.add), reads=["ang", "kf"], writes=["ang"])
                P.dve(lambda e: e.tensor_scalar(out=ang, in0=ang, scalar1=3.1415925, scalar2=-3.1415925, op0=ALU.min, op1=ALU.max),
                      reads=["ang"], writes=["ang"])
                P.act(lambda e, T=T, cs=cs: e.activation(out=T[:, cs], in_=ang, func=AF.Sin), reads=["ang"], writes=[(tres, n)])

        if stop_at <= 1.5:
            return
        sqb = AR.alloc([128, 5, 512], BF16)
        rqb = AR.alloc([128, 2, 512], F32)
        tmpA = AR.alloc([64, 512], F32)
        tmpB = AR.alloc([64, 512], F32)
        wA = []
        for i in range(4):
            w, r = load_piece(layer, i, 8 * 512 if i < 3 else 8 * 384)
            wA.append((wview(w, 8, 512 if i < 3 else 384), r))

        def wA_chunk(c):
            w, r = wA[c // 4]
            return (lambda j: w[:, j, (c % 4) * 128:(c % 4) * 128 + 128]), r

        if stop_at <= 1.7:
            return
        bank_ctr = [0]

        def next_bank():
            b = 2 + (bank_ctr[0] % 6)
            bank_ctr[0] += 1
            return bank(b)

        def proj_chunk(c, n):
            pb, pres = next_bank()
            lw, wres = wA_chunk(c)
            for j in range(8):
                P.pe(lambda e, j=j, pb=pb, lw=lw: e.matmul(pb, lhsT=lw(j), rhs=actT[:, j, n * 512:(n + 1) * 512],
                                                        start=(j == 0), stop=(j == 7)),
                     reads=[hT_res(j, n), wres], writes=[pres])
            return pb, pres

        for n in range(4):
            if layer == 0:
                rope_chunk(n)
            cs = slice(n * 512, (n + 1) * 512)
            for c in range(5):
                pb, pres = proj_chunk(c, n)
                if stop_at <= 1.75:
                    return
                gc = gb + 32 + c
                P.act(lambda e, c=c, pb=pb: e.activation(out=sqb[:, c, :], in_=pb, func=AF.Square), reads=[pres], writes=[("sqb", c)])
                P.dve(lambda e, c=c, pb=pb, gc=gc, cs=cs: e.tensor_scalar(out=craw[:, c, cs], in0=pb, scalar1=gcol[:, gc:gc + 1], scalar2=None,
                                                                        op0=ALU.mult), reads=[pres, "gcol"], writes=[("craw", c, n)])
                if stop_at <= 1.8:
                    return
            for which, (c0, c1, dim) in enumerate(((0, 3, 384), (3, 5, 256))):
                pb, pres = next_bank()
                for c in range(c0, c1):
                    P.pe(lambda e, c=c, pb=pb, c0=c0, c1=c1: e.matmul(pb, lhsT=ones_bf, rhs=sqb[:, c, :], start=(c == c0), stop=(c == c1 - 1)),
                         reads=["ones", ("sqb", c)], writes=[pres])
                rres = ("rqb", which)
                P.dve(lambda e, pb=pb, which=which, dim=dim: e.tensor_scalar(out=rqb[:, which, :], in0=pb, scalar1=1.0 / dim, scalar2=EPS,
                                                                           op0=ALU.mult, op1=ALU.add), reads=[pres], writes=[rres])
                P.act(lambda e, which=which: e.activation(out=rqb[:, which, :], in_=rqb[:, which, :], func=AF.Ln), reads=[rres], writes=[rres])
                P.act(lambda e, which=which: e.activation(out=rqb[:, which, :], in_=rqb[:, which, :], func=AF.Exp, scale=-0.5),
                      reads=[rres], writes=[rres])
                for c in range(c0, c1):
                    P.dve(lambda e, c=c, which=which, cs=cs: e.tensor_tensor(out=craw[:, c, cs], in0=craw[:, c, cs], in1=rqb[:, which, :],
                                                                           op=ALU.mult), reads=[("craw", c, n), rres], writes=[("craw", c, n)])
            if stop_at <= 1.9:
                return
            pa, pares = proj_chunk(5, n)
            pbb, pbres = proj_chunk(6, n)
            P.dve(lambda e, pa=pa, cs=cs: e.tensor_tensor(out=tmpA, in0=pa[0:64, :], in1=TC[:, cs], op=ALU.mult),
                  reads=[pares, ("TC", n)], writes=["tmpA"])
            P.dve(lambda e, pbb=pbb, cs=cs: e.tensor_tensor(out=tmpB, in0=pbb[0:64, :], in1=TS[:, cs], op=ALU.mult),
                  reads=[pbres, ("TS", n)], writes=["tmpB"])
            P.dve(lambda e, cs=cs: e.tensor_tensor(out=kpe[:, cs], in0=tmpA, in1=tmpB, op=ALU.add),
                  reads=["tmpA", "tmpB"], writes=[("kpe", n)])
            if stop_at <= 1.95:
                return
            for i in range(4):
                pb, pres = proj_chunk(7 + i, n)
                evac_scaled(qdT[:, i, cs], pb, None, reads=[pres], writes=[("qdT", i, n)])
                pb, pres = proj_chunk(11 + i, n)
                evac_scaled(kdT[:, i, cs], pb, None, reads=[pres], writes=[("kdT", i, n)])
        wV, wVr = load_piece(layer, 4, 8 * 512)
        wVv = wview(wV, 8, 512)
        for i in range(4):
            for n in range(4):
                cs = slice(n * 512, (n + 1) * 512)
                pb, pres = next_bank()
                for j in range(8):
                    P.pe(lambda e, j=j, pb=pb, i=i, cs=cs: e.matmul(pb, lhsT=wVv[:, j, i * 128:(i + 1) * 128], rhs=actT[:, j, cs],
                                                                  start=(j == 0), stop=(j == 7)),
                         reads=[hT_res(j, n), wVr], writes=[pres])
                evac_scaled(vdT[:, i, cs], pb, None, reads=[pres], writes=[("vdT", i, n)])
        prefetch(layer, 5, 3 * 1024)
        prefetch(layer, 6, 2 * 1024)
        prefetch(layer, 7, 8 * 512)
        prefetch(layer, 8, 8 * 512)
        AR.release(m1)
        if "rqb" in dumps:
            dump("rqb", rqb, [128, 2, 512], F32, [("rqb", 0), ("rqb", 1)])
            dump("sqb", sqb, [128, 5, 512], BF16, [("sqb", c) for c in range(5)])
        if "craw" in dumps:
            dump("craw", craw, [128, 5, S], BF16, [("craw", c, n) for c in range(5) for n in range(4)])
            dump("kpe", kpe, [64, S], BF16, [("kpe", n) for n in range(4)])
            dump("qdT", qdT, [128, 4, S], BF16, [("qdT", i, n) for i in range(4) for n in range(4)])
            dump("vdT", vdT, [128, 4, S], BF16, [("vdT", i, n) for i in range(4) for n in range(4)])

        if stop_at <= 2:
            return
        P.barrier()
        m2 = AR.mark()
        btab = AR.alloc([128, 3, 4, 512], BF16)
        P.dma("pool", "btab", lambda e: e.dma_start(out=btab.rearrange("p a b c -> p (a b c)").rearrange("p (a b) -> p a b", b=1024),
                                                  in_=btab_in.rearrange("p (a b) -> p a b", b=1024)), writes=["btab"])
        P.dve(lambda e: e.tensor_scalar(out=btab, in0=btab, scalar1=8.0, scalar2=None, op0=ALU.mult), reads=["btab"], writes=["btab"])
        ND = AR.alloc([128, 2, S], F32)
        vblk = AR.alloc([128, 4, 256], BF16)
        pT = AR.alloc([128, 3, 512], BF16)
        qpad = AR.alloc([128, 2, 2, S], BF16)
        P.pool(lambda e: e.memset(qpad, 0.0), writes=[("qpad", 0), ("qpad", 1)])
        recj = junk.bitcast(F32)[0:64, :]
        P.pool(lambda e: e.memset(vblk, 1.0), writes=[("vblk", s) for s in range(4)])
        PSb = [PS[0][:, 0:512].bitcast(BF16), PS[0][:, 512:1024].bitcast(BF16)]
        def dil_blocks(i):
            blks = []
            for p, dil in enumerate(DILS):
                ncb = 16 // dil
                for r in range(dil):
                    for c in range(ncb):
                        blks.append((p, dil, r, c))
            return blks

        gctr = [0]

        def front(i, blk, st_):
            p, dil, r, c = blk
            g_ = gctr[0]
            gctr[0] += 1
            t0 = r + dil * 128 * c
            toks = slice(t0, t0 + dil * 127 + 1, dil)
            if dil == 1:
                chunks_touched = [c // 4]
            elif dil == 4:
                chunks_touched = [c]
            else:
                chunks_touched = [0, 1, 2, 3]
            slot = g_ % 4
            par = g_ % 2
            s3 = g_ % 3
            P.pe(lambda e: e.transpose(out=PSb[par][:, 0:128], in_=vdT[:, i, toks], identity=ident_bf),
                 reads=[("vdT", i, n) for n in chunks_touched] + ["identb"], writes=[("pb", par)])
            if g_ % 2 == 0:
                P.act(lambda e: e.copy(out=vblk[:, slot, :].rearrange("p (h x) -> p h x", h=2)[:, :, 0:64],
                                       in_=PSb[par][:, 0:128].rearrange("p (h x) -> p h x", h=2)),
                      reads=[("pb", par)], writes=[("vblk", slot)])
            else:
                P.dve(lambda e: e.tensor_copy(out=vblk[:, slot, :].rearrange("p (h x) -> p h x", h=2)[:, :, 0:64],
                                              in_=PSb[par][:, 0:128].rearrange("p (h x) -> p h x", h=2)),
                      reads=[("pb", par)], writes=[("vblk", slot)])
            Sb, Sres = bank(2 + s3)
            kbs = [1] if c == 0 else [0, 1]
            for kb in kbs:
                kt0 = t0 if kb == 1 else t0 - dil * 128
                ktoks = slice(kt0, kt0 + dil * 127 + 1, dil)
                P.pe(lambda e, ktoks=ktoks, kb=kb: e.matmul(Sb[:, kb * 256:kb * 256 + 256], lhsT=kdT[:, i, ktoks], rhs=qpad[:, i % 2, :, toks],
                                                           start=True, stop=False),
                     reads=[("kdT", i, n) for n in range(4)] + [("qpad", i % 2)], writes=[Sres])
                P.pe(lambda e, kb=kb: e.matmul(Sb[:, kb * 256:kb * 256 + 256], lhsT=ident_bf, rhs=btab[:, p, i, kb * 256:kb * 256 + 256],
                                              start=False, stop=True),
                     reads=["identb", "btab"], writes=[Sres])
            lo = 256 if c == 0 else 0
            P.act(lambda e: e.activation(out=pT[:, s3, lo:512], in_=Sb[:, lo:512], func=AF.Exp, scale=0.125),
                  reads=[Sres], writes=[("pT", s3)])
            st_.update(dict(toks=toks, chunks=chunks_touched, slot=slot, s3=s3, kbs=kbs, p=p, c=c))

        def back(i, st_, prev_slot):
            s3 = st_["s3"]
            kbs = st_["kbs"]
            slot = st_["slot"]
            Ob, Ores = bank(5 + s3)
            for hh in range(2):
                for kb in kbs:
                    vs = slot if kb == 1 else prev_slot
                    col = (kb * 2 + hh) * 128
                    P.pe(lambda e, hh=hh, vs=vs, col=col, kb=kb: e.matmul(
                            Ob[:, hh * 128:(hh + 1) * 128], lhsT=vblk[:, vs, hh * 128:(hh + 1) * 128], rhs=pT[:, s3, col:col + 128],
                            start=(kb == kbs[0]), stop=(kb == 1)),
                         reads=[("vblk", vs), ("pT", s3)], writes=[Ores])
            ndv = ND[:, :, st_["toks"]]
            obv = Ob[:, 0:256].rearrange("p (h x) -> p h x", h=2)
            nd_res = [("ND", n) for n in st_["chunks"]]
            if st_["p"] == 0:
                P.dve(lambda e: e.tensor_copy(out=ndv, in_=obv), reads=[Ores], writes=nd_res)
            else:
                P.dve(lambda e: e.tensor_tensor(out=ndv, in0=ndv, in1=obv, op=ALU.add), reads=[Ores] + nd_res, writes=nd_res)

        DDEPTH = 2

        def load_qpad(i):
            bq = i % 2
            P.dve(lambda e: e.tensor_copy(out=qpad[0:64, bq, 0, :], in_=qdT[0:64, i, :]),
                  reads=[("qdT", i, n) for n in range(4)], writes=[("qpad", bq)])
            P.act(lambda e: e.copy(out=qpad[64:128, bq, 1, :], in_=qdT[64:128, i, :]),
                  reads=[("qdT", i, n) for n in range(4)], writes=[("qpad", bq)])

        def norm_chunk(i, n):
            cs = slice(n * 512, (n + 1) * 512)
            for hh in range(2):
                P.act(lambda e, hh=hh: e.activation(out=recj, in_=ND[64:128, hh, cs], func=AF.Ln), reads=[("ND", n)], writes=["junk"])
                P.act(lambda e: e.activation(out=recj, in_=recj, func=AF.Exp, scale=-1.0), reads=["junk"], writes=["junk"])
                P.dve(lambda e, hh=hh: e.tensor_tensor(out=actT[hh * 64:hh * 64 + 64, 4 + i, cs], in0=ND[0:64, hh, cs], in1=recj, op=ALU.mult),
                      reads=[("ND", n), "junk"], writes=[hT_res(4 + i, n)])

        load_qpad(0)
        for i in range(4):
            blks = dil_blocks(i)
            states = [dict() for _ in blks]
            for bi in range(min(DDEPTH, len(blks))):
                front(i, blks[bi], states[bi])
            for bi in range(len(blks)):
                if bi + DDEPTH < len(blks):
                    front(i, blks[bi + DDEPTH], states[bi + DDEPTH])
                if i > 0 and bi < 16 and bi % 4 == 0:
                    norm_chunk(i - 1, bi // 4)
                if bi == 24 and i + 1 < 4:
                    load_qpad(i + 1)
                prev_slot = states[bi - 1]["slot"] if states[bi]["c"] > 0 else None
                back(i, states[bi], prev_slot)
        for n in range(4):
            norm_chunk(3, n)
        AR.release(m2)

        if stop_at <= 3:
            return
        AR.release(m1d)
        P.barrier()
        wq, wqr = load_piece(layer, 5, 3 * 1024)
        wk, wkr = load_piece(layer, 6, 2 * 1024)
        wqv = wview(wq, 3, 1024)
        wkv = wview(wk, 2, 1024)
        q96 = AR.alloc([128, 2, S], BF16)
        k96 = AR.alloc([128, 2, S], BF16)
        vp = AR.alloc([128, NT, 8, 128], BF16)
        pTm = AR.alloc([128, 6, 512], BF16)
        recm = AR.alloc([64, 2, 512], F32)
        tA = AR.alloc([64, 512], F32)
        tB = AR.alloc([64, 512], F32)
        P.pool(lambda e: e.memset(vp, 1.0), writes=[("vp", tt) for tt in range(NT)])
        for hh in range(2):
            for n in range(4):
                cs = slice(n * 512, (n + 1) * 512)
                P.dve(lambda e, hh=hh, cs=cs: e.tensor_copy(out=k96[64:96, hh, cs], in_=kpe[0:32, cs]),
                      reads=[("kpe", n)], writes=[("k96", hh, n)])
        for tt in range(NT):
            pb, pres = next_bank()
            for c in range(2):
                P.pe(lambda e, c=c, pb=pb, tt=tt: e.matmul(pb, lhsT=craw[:, 3 + c, tt * 128:(tt + 1) * 128], rhs=wkv[:, c, 512:1024],
                                                         start=(c == 0), stop=(c == 1)),
                     reads=[("craw", 3 + c, tt // 4), wkr], writes=[pres])
            evac_scaled(vp[:, tt, :, 0:64], pb.rearrange("p (h x) -> p h x", h=8), None, reads=[pres], writes=[("vp", tt)])
        scale_m = float(96 ** -0.5)
        pctr = [0]
        for i in range(4):
            for n in range(4):
                cs = slice(n * 512, (n + 1) * 512)
                pb, pres = next_bank()
                for c in range(3):
                    P.pe(lambda e, c=c, pb=pb, cs=cs, i=i: e.matmul(pb, lhsT=wqv[:, c, i * 128:(i + 1) * 128], rhs=craw[:, c, cs],
                                                                  start=(c == 0), stop=(c == 2)), reads=[("craw", c, n), wqr], writes=[pres])
                for hh in range(2):
                    evac_scaled(q96[0:64, hh, cs], pb[hh * 64:hh * 64 + 64, :], None, reads=[pres], writes=[("q96", hh, n)])
                pb, pres = next_bank()
                for c in range(2):
                    P.pe(lambda e, c=c, pb=pb, cs=cs, i=i: e.matmul(pb, lhsT=wkv[:, c, i * 128:(i + 1) * 128], rhs=craw[:, 3 + c, cs],
                                                                  start=(c == 0), stop=(c == 1)), reads=[("craw", 3 + c, n), wkr], writes=[pres])
                for hh in range(2):
                    evac_scaled(k96[0:64, hh, cs], pb[hh * 64:hh * 64 + 64, :], None, reads=[pres], writes=[("k96", hh, n)])
                pa, pares = next_bank()
                pbb, pbres = next_bank()
                for c in range(3):
                    P.pe(lambda e, c=c, pa=pa, cs=cs, i=i: e.matmul(pa[0:64, :], lhsT=wqv[:, c, 512 + i * 64:512 + i * 64 + 64], rhs=craw[:, c, cs],
                                                                  start=(c == 0), stop=(c == 2)), reads=[("craw", c, n), wqr], writes=[pares])
                for c in range(3):
                    P.pe(lambda e, c=c, pbb=pbb, cs=cs, i=i: e.matmul(pbb[0:64, :], lhsT=wqv[:, c, 768 + i * 64:768 + i * 64 + 64], rhs=craw[:, c, cs],
                                                                    start=(c == 0), stop=(c == 2)), reads=[("craw", c, n), wqr], writes=[pbres])
                P.dve(lambda e, pa=pa, cs=cs: e.tensor_tensor(out=tA, in0=pa[0:64, :], in1=TC[:, cs], op=ALU.mult),
                      reads=[pares, ("TC", n)], writes=["tA"])
                P.dve(lambda e, pbb=pbb, cs=cs: e.tensor_tensor(out=tB, in0=pbb[0:64, :], in1=TS[:, cs], op=ALU.mult),
                      reads=[pbres, ("TS", n)], writes=["tB"])
                for hh in range(2):
                    P.dve(lambda e, cs=cs, hh=hh: e.tensor_tensor(out=q96[64:96, hh, cs], in0=tA[hh * 32:hh * 32 + 32, :],
                                                                in1=tB[hh * 32:hh * 32 + 32, :], op=ALU.add),
                          reads=["tA", "tB"], writes=[("q96", hh, n)])
            items = []
            for hh in range(2):
                for n in range(4):
                    nkt = 4 * n + 4
                    for kt in range(nkt):
                        items.append((hh, n, kt, nkt))
            DEPTH = 4
            sts = [None] * len(items)

            def mfront(j, i=i, items=items, sts=sts):
                hh, n, kt, nkt = items[j]
                mq = max(0, kt - 4 * n)
                c0 = mq * 128
                ks = slice(kt * 128, (kt + 1) * 128)
                qs = slice(n * 512 + c0, (n + 1) * 512)
                Sb, Sres = next_bank()
                P.pe(lambda e: e.matmul(Sb[:, c0:512], lhsT=k96[0:96, hh, ks], rhs=q96[0:96, hh, qs], start=True, stop=True),
                     reads=[("k96", hh, kt // 4), ("q96", hh, n)], writes=[Sres])
                pq = pctr[0] % 6
                pctr[0] += 1
                P.act(lambda e: e.activation(out=pTm[:, pq, c0:512], in_=Sb[:, c0:512], func=AF.Exp, scale=scale_m),
                      reads=[Sres], writes=[("pTm", pq)])
                if kt >= 4 * n:
                    P.pool(lambda e: e.affine_select(out=pTm[:, pq, c0:c0 + 128], in_=pTm[:, pq, c0:c0 + 128],
                                                     pattern=[[1, 128]], compare_op=ALU.is_ge, fill=0.0, base=0,
                                                     channel_multiplier=-1),
                           reads=[("pTm", pq)], writes=[("pTm", pq)])
                sts[j] = (pq, c0)

            def mback(j, i=i, items=items, sts=sts):
                hh, n, kt, nkt = items[j]
                pq, c0 = sts[j]
                hs = slice(hh * 64, hh * 64 + 64)
                cs = slice(n * 512, (n + 1) * 512)
                Ob, Ores = bank(n % 2)
                P.pe(lambda e: e.matmul(Ob[:, c0:512], lhsT=vp[:, kt, 2 * i + hh, :], rhs=pTm[:, pq, c0:512],
                                        start=(kt == 0), stop=(kt == nkt - 1)),
                     reads=[("vp", kt), ("pTm", pq)], writes=[Ores])
                if kt == nkt - 1:
                    q2 = n % 2
                    rres = ("recm", q2)
                    P.act(lambda e: e.activation(out=recm[:, q2, :], in_=Ob[64:128, :], func=AF.Ln), reads=[Ores], writes=[rres])
                    P.act(lambda e: e.activation(out=recm[:, q2, :], in_=recm[:, q2, :], func=AF.Exp, scale=-1.0), reads=[rres], writes=[rres])
                    P.dve(lambda e: e.tensor_tensor(out=actT[hs, i, cs], in0=Ob[0:64, :], in1=recm[:, q2, :], op=ALU.mult),
                          reads=[Ores, rres], writes=[hT_res(i, n)])

            for j in range(min(DEPTH, len(items))):
                mfront(j)
            for j in range(len(items)):
                if j + DEPTH < len(items):
                    mfront(j + DEPTH)
                mback(j)
        phase_out(actT, actT_tile_res, 8, layer, 7, 0, srcT, srcR, dstT, dstR, next_norm=next_norm_of(sub),
                  pre=[(layer, 9, 8 * 512), (layer, 10, 8 * 512)], nobarrier=True)
        AR.release(m0)

    def xattn(layer, sub):
        srcT, srcR = x_src(sub)
        dstT, dstR = x_dst(sub)
        gb = layer * GC
        m0 = AR.mark()
        memT = AR.alloc([128, 8, 256], BF16)
        phase_norm(lambda tt: mem_in[tt], lambda tt: ("memin", tt), 2, gb + 24, memT, lambda j, g: ("memT", j), width_tiles=2)
        if sub == 0:
            phase_norm(srcT, srcR, NT, gb + 8, actT, hT_res)
        qT = AR.alloc([128, 8, S], BF16)
        kT = AR.alloc([128, 8, 256], BF16)
        vx = AR.alloc([128, 2, DM], BF16)
        pTx = AR.alloc([128, 4, 512], BF16)
        recx = AR.alloc([128, 2, 512], F32)
        bctr = [0]

        def nb():
            b = 2 + (bctr[0] % 6)
            bctr[0] += 1
            return bank(b)

        for hf in range(2):
            w, wr = load_piece(layer, 9 + hf, 8 * 512)
            wv = wview(w, 8, 512)
            for mc in range(4):
                m = hf * 4 + mc
                for n in range(4):
                    cs = slice(n * 512, (n + 1) * 512)
                    pb, pres = nb()
                    for j in range(8):
                        P.pe(lambda e, j=j, pb=pb, mc=mc, cs=cs, wv=wv: e.matmul(pb, lhsT=wv[:, j, mc * 128:(mc + 1) * 128], rhs=actT[:, j, cs],
                                                                               start=(j == 0), stop=(j == 7)), reads=[hT_res(j, n), wr], writes=[pres])
                    evac_scaled(qT[:, m, cs], pb, None, reads=[pres], writes=[("qT", m, n)])
        for hf in range(2):
            w, wr = load_piece(layer, 11 + hf, 8 * 512)
            wv = wview(w, 8, 512)
            for mc in range(4):
                m = hf * 4 + mc
                pb, pres = nb()
                for j in range(8):
                    P.pe(lambda e, j=j, pb=pb, mc=mc, wv=wv: e.matmul(pb[:, 0:256], lhsT=wv[:, j, mc * 128:(mc + 1) * 128], rhs=memT[:, j, :],
                                                                    start=(j == 0), stop=(j == 7)), reads=[("memT", j), wr], writes=[pres])
                evac_scaled(kT[:, m, :], pb[:, 0:256], None, reads=[pres], writes=[("kT", m)])
        for hf in range(2):
            w, wr = load_piece(layer, 13 + hf, 8 * 512)
            wv = wview(w, 8, 512)
            for mt in range(2):
                pb, pres = nb()
                for j in range(8):
                    P.pe(lambda e, j=j, pb=pb, mt=mt, wv=wv: e.matmul(pb, lhsT=memT[:, j, mt * 128:(mt + 1) * 128], rhs=wv[:, j, :],
                                                                    start=(j == 0), stop=(j == 7)), reads=[("memT", j), wr], writes=[pres])
                evac_scaled(vx[:, mt, hf * 512:(hf + 1) * 512], pb, None, reads=[pres], writes=[("vx", mt, hf)])
        prefetch(layer, 15, 8 * 512)
        prefetch(layer, 16, 8 * 512)
        pctr = [0]
        groups = [(h, n) for h in range(4) for n in range(4)]
        xst = [None] * len(groups)

        def xfront(g):
            h, n = groups[g]
            cs = slice(n * 512, (n + 1) * 512)
            pqs = []
            for mt in range(2):
                Sb, Sres = nb()
                for c in range(2):
                    P.pe(lambda e, Sb=Sb, c=c, mt=mt: e.matmul(Sb, lhsT=kT[:, 2 * h + c, mt * 128:(mt + 1) * 128], rhs=qT[:, 2 * h + c, cs],
                                                             start=(c == 0), stop=(c == 1)),
                         reads=[("kT", 2 * h + c), ("qT", 2 * h + c, n)], writes=[Sres])
                pq = pctr[0] % 4
                pctr[0] += 1
                pqs.append(pq)
                P.act(lambda e, Sb=Sb, pq=pq: e.activation(out=pTx[:, pq, :], in_=Sb, func=AF.Exp, scale=1.0 / 16.0),
                      reads=[Sres], writes=[("pTx", pq)])
            xst[g] = pqs

        def xback(g):
            h, n = groups[g]
            cs = slice(n * 512, (n + 1) * 512)
            pqs = xst[g]
            Db, Dres = nb()
            for mt in range(2):
                P.pe(lambda e, mt=mt, pq=pqs[mt]: e.matmul(Db, lhsT=ones_bf, rhs=pTx[:, pq, :], start=(mt == 0), stop=(mt == 1)),
                     reads=["ones", ("pTx", pqs[mt])], writes=[Dres])
            q2 = g % 2
            rres = ("recx", q2)
            P.act(lambda e: e.activation(out=recx[:, q2, :], in_=Db, func=AF.Ln), reads=[Dres], writes=[rres])
            P.act(lambda e: e.activation(out=recx[:, q2, :], in_=recx[:, q2, :], func=AF.Exp, scale=-1.0), reads=[rres], writes=[rres])
            for c in range(2):
                Ob, Ores = nb()
                m = 2 * h + c
                for mt in range(2):
                    P.pe(lambda e, Ob=Ob, mt=mt, m=m, pq=pqs[mt]: e.matmul(Ob, lhsT=vx[:, mt, m * 128:(m + 1) * 128], rhs=pTx[:, pq, :],
                                                                         start=(mt == 0), stop=(mt == 1)),
                         reads=[("vx", mt, m // 4), ("pTx", pqs[mt])], writes=[Ores])
                P.dve(lambda e, Ob=Ob, m=m: e.tensor_tensor(out=actT[:, m, cs], in0=Ob, in1=recx[:, q2, :], op=ALU.mult),
                      reads=[Ores, rres], writes=[hT_res(m, n)])

        xfront(0)
        for g in range(len(groups)):
            if g + 1 < len(groups):
                xfront(g + 1)
            xback(g)
        phase_out(actT, actT_tile_res, 8, layer, 15, 1, srcT, srcR, dstT, dstR, next_norm=next_norm_of(sub),
                  pre=[(layer, 17, 8 * 512), (layer, 18, 8 * 512)], nobarrier=True)
        AR.release(m0)

    def mlp(layer, sub):
        srcT, srcR = x_src(sub)
        dstT, dstR = x_dst(sub)
        gb = layer * GC
        if sub == 0:
            phase_norm(srcT, srcR, NT, gb + 16, actT, hT_res)
        m0 = AR.mark()
        hid = AR.alloc([128, 2, 4, S], BF16)
        yacc = AR.alloc([128, NT, DM], F32)
        rl = AR.alloc([128, 2, 512], F32)
        bctr = [0]

        def nb():
            b = 2 + (bctr[0] % 6)
            bctr[0] += 1
            return bank(b)

        def up(g):
            w, wr = load_piece(layer, 17 + g, 8 * 512)
            wv = wview(w, 8, 512)
            hb = g % 2
            for fc in range(4):
                for n in range(4):
                    cs = slice(n * 512, (n + 1) * 512)
                    pb, pres = nb()
                    for j in range(8):
                        P.pe(lambda e, j=j, pb=pb, fc=fc, cs=cs, wv=wv: e.matmul(pb, lhsT=wv[:, j, fc * 128:(fc + 1) * 128], rhs=actT[:, j, cs],
                                                                               start=(j == 0), stop=(j == 7)), reads=[hT_res(j, n), wr], writes=[pres])
                    q2 = bctr[0] % 2
                    P.act(lambda e, pb=pb, q2=q2: e.activation(out=rl[:, q2, :], in_=pb, func=AF.Relu), reads=[pres], writes=[("rl", q2)])
                    P.dve(lambda e, pb=pb, q2=q2, hb=hb, fc=fc, cs=cs: e.tensor_tensor(out=hid[:, hb, fc, cs], in0=pb, in1=rl[:, q2, :], op=ALU.mult),
                          reads=[pres, ("rl", q2)], writes=[("hid", hb, fc, n)])

        def down(g):
            hb = g % 2
            ws = []
            for hf in range(2):
                w, wr = load_piece(layer, 25 + 2 * g + hf, 4 * 512)
                ws.append((wview(w, 4, 512), wr))
            if g == 7:
                prefetch(layer + 1, 0, 8 * 512)
                prefetch(layer + 1, 1, 8 * 512)
            for tt in range(NT):
                for hf in range(2):
                    pb, pres = nb()
                    wv, wr = ws[hf]
                    for k in range(4):
                        P.pe(lambda e, k=k, pb=pb, tt=tt, wv=wv, hb=hb: e.matmul(pb, lhsT=hid[:, hb, k, tt * 128:(tt + 1) * 128], rhs=wv[:, k, :],
                                                                               start=(k == 0), stop=(k == 3)),
                             reads=[("hid", hb, k, tt // 4), wr], writes=[pres])
                    ya = yacc[:, tt, hf * 512:(hf + 1) * 512]
                    yr = ("yacc", tt, hf)
                    if g == 0:
                        P.act(lambda e, ya=ya, pb=pb: e.copy(out=ya, in_=pb), reads=[pres], writes=[yr])
                    else:
                        P.dve(lambda e, ya=ya, pb=pb: e.tensor_tensor(out=ya, in0=ya, in1=pb, op=ALU.add), reads=[pres, yr], writes=[yr])
                if g == 7:
                    out_tile(tt)

        up(0)
        out_tile = out_finish = None
        for g in range(8):
            if g + 1 < 8:
                up(g + 1)
            if g == 7:
                out_tile, out_finish = phase_out(None, None, 0, layer, None, 2, srcT, srcR, dstT, dstR,
                                                 acc=(yacc, lambda tt: [("yacc", tt, 0), ("yacc", tt, 1)]), next_norm=next_norm_of(sub),
                                                 pre=(), deferred=True)
            down(g)
        out_finish()
        AR.release(m0)

    subs = []
    for l in range(NL):
        subs += [(mixer, l), (xattn, l), (mlp, l)]
    for sub in range(n_sub):
        fn, l = subs[sub]
        fn(l, sub)
    if stop_at < 99:
        P.dma("sp", "dbgcopy", lambda e: e.dma_start(out=out[0], in_=x_in[0]), writes=["dbgout"])
    P.emit(st)
    st.close()
    return nc, P, AR, dump_out


_PROGRAM_CACHE = {}


def kernel(**inputs):
    inputs = {k: np.asarray(v) for k, v in inputs.items()}
    B = inputs["x"].shape[0]
    shared = prep_shared(inputs)
    if "full" not in _PROGRAM_CACHE:
        _PROGRAM_CACHE["full"] = build_program(6)[0]
    nc = _PROGRAM_CACHE["full"]
    in_maps = []
    for b in range(B):
        m = dict(shared)
        m["x"] = np.ascontiguousarray(inputs["x"][b].reshape(NT, 128, DM))
        m["mem"] = np.ascontiguousarray(inputs["mem"][b].reshape(2, 128, DM))
        m["pos"] = np.ascontiguousarray(np.broadcast_to(inputs["positions"][b].astype(np.int32)[None, :], (128, S)))
        in_maps.append(m)
    res = run_bass_kernel_spmd(nc, in_maps, core_ids=list(range(B)))
    outs = [np.asarray(r["out"]).reshape(S, DM) for r in res.results]
    return np.stack(outs, axis=0).astype(np.float32)
```
